# Optimizing a Trainium2 kernel written in Bass

```python
import jax, jax.numpy as jnp
from jax import lax
import numpy as np

D_MODEL = 4096
BATCH = 4
SEQ = 4096
DEPTH = 1

GRID_W = 64
CTX_LEN = 256
NA_HEADS = 16
HEAD_DIM = 128
D_NA = NA_HEADS * HEAD_DIM
NA_KH_MAX = 8
NA_KW = 16
D_SGU = D_MODEL - D_NA
SGU_GROUPS = 4
SGU_GROUP_DIM = D_SGU // SGU_GROUPS
SGU_CHUNK = 128
D_IN = 3 * D_NA + 2 * D_SGU
D_FF = 4 * D_MODEL
N_MOD = 6
NORM_EPS = 1e-6

kernel_name = "hybrid_na_sgu_dit_layer"


def rmsnorm(x, w):
    xf = x.astype(jnp.float32)
    y = xf * lax.rsqrt(jnp.mean(xf * xf, axis=-1, keepdims=True) + NORM_EPS)
    return (y * w.astype(jnp.float32)).astype(x.dtype)


def adaln(cvec, w_ada, b_ada):
    m = (jax.nn.silu(cvec) @ w_ada + b_ada)[..., None, :]
    return jnp.split(m, N_MOD, axis=-1)


def modulate(h, shift, scale):
    return h * (1 + scale) + shift


def split_proj(proj):
    return jnp.split(proj, [D_NA, 2 * D_NA, 3 * D_NA, 3 * D_NA + D_SGU], axis=-1)


def to_heads(t):
    b, l, _ = t.shape
    return t.reshape(b, l, NA_HEADS, HEAD_DIM)


def neighbourhood_attention(q, k, v, k_ctx, v_ctx, rpb):
    b, s, h, dh = q.shape
    rows = s // GRID_W
    kh = min(NA_KH_MAX, rows)
    scale = dh ** -0.5
    qg = (q * scale).reshape(b, rows, GRID_W, h, dh)
    kg = k.reshape(b, rows, GRID_W, h, dh)
    vg = v.reshape(b, rows, GRID_W, h, dh)
    cols = jnp.arange(GRID_W)
    col_start = jnp.clip(cols - NA_KW // 2, 0, GRID_W - NA_KW)
    key_cols = col_start[:, None] + jnp.arange(NA_KW)[None, :]
    col_off = key_cols - cols[:, None] + (NA_KW - 1)
    n_win = kh * NA_KW

    def row_block(r):
        r0 = jnp.clip(r - kh // 2, 0, rows - kh)
        q_r = lax.dynamic_index_in_dim(qg, r, axis=1, keepdims=False)
        k_rows = lax.dynamic_slice_in_dim(kg, r0, kh, axis=1)
        v_rows = lax.dynamic_slice_in_dim(vg, r0, kh, axis=1)
        k_win = k_rows[:, :, key_cols]
        v_win = v_rows[:, :, key_cols]
        s_win = jnp.einsum('bqhd,biqjhd->bhqij', q_r, k_win)
        row_off = r0 + jnp.arange(kh) - r + (NA_KH_MAX - 1)
        bias = rpb[:, row_off][:, :, col_off]
        s_win = s_win + jnp.transpose(bias, (0, 2, 1, 3))[None]
        s_ctx = jnp.einsum('bqhd,bchd->bhqc', q_r, k_ctx)
        scores = jnp.concatenate([s_win.reshape(b, h, GRID_W, n_win), s_ctx], axis=-1)
        p = jax.nn.softmax(scores.astype(jnp.float32), axis=-1).astype(v.dtype)
        p_win = p[..., :n_win].reshape(b, h, GRID_W, kh, NA_KW)
        p_ctx = p[..., n_win:]
        return (jnp.einsum('bhqij,biqjhd->bqhd', p_win, v_win)
                + jnp.einsum('bhqc,bchd->bqhd', p_ctx, v_ctx))

    out = lax.map(row_block, jnp.arange(rows))
    return jnp.transpose(out, (1, 0, 2, 3, 4)).reshape(b, s, h * dh)


def context_attention(q, k, v):
    b, l, h, dh = q.shape
    s = jnp.einsum('bqhd,bkhd->bhqk', q * dh ** -0.5, k)
    p = jax.nn.softmax(s.astype(jnp.float32), axis=-1).astype(v.dtype)
    return jnp.einsum('bhqk,bkhd->bqhd', p, v).reshape(b, l, h * dh)


def spatial_gating(u, g, w_s, b_s, norm_w):
    b, l, _ = u.shape
    gn = rmsnorm(g, norm_w).reshape(b, l // SGU_CHUNK, SGU_CHUNK, SGU_GROUPS, SGU_GROUP_DIM)
    mixed = jnp.einsum('gpq,bnqgc->bnpgc', w_s, gn) + jnp.transpose(b_s)[None, None, :, :, None]
    return u * mixed.reshape(b, l, D_SGU)


def merge_groups(o_na, o_sgu, gn_na, gn_sgu, w_out):
    return jnp.concatenate([rmsnorm(o_na, gn_na), rmsnorm(o_sgu, gn_sgu)], axis=-1) @ w_out


def squared_relu_mlp(h, w1, w2):
    return jnp.square(jax.nn.relu(h @ w1)) @ w2


def setup_inputs(seed: int = 0) -> dict:
    key = jax.random.key(seed)
    ks = jax.random.split(key, 20)
    f32 = jnp.float32
    nrm = lambda k, shape, s: jax.random.normal(k, shape, f32) * s
    gain = lambda k, shape: 1.0 + 0.01 * jax.random.normal(k, shape, f32)
    return {
        "x": nrm(ks[0], (BATCH, SEQ, D_MODEL), 1.0),
        "c": nrm(ks[1], (BATCH, D_MODEL), 1.0),
        "ctx": nrm(ks[2], (BATCH, CTX_LEN, D_MODEL), 1.0),
        "c_ctx": nrm(ks[3], (D_MODEL,), 1.0),
        "w_ada": nrm(ks[4], (DEPTH, D_MODEL, N_MOD * D_MODEL), D_MODEL ** -0.5),
        "b_ada": nrm(ks[5], (DEPTH, N_MOD * D_MODEL), 0.01),
        "norm1_w": gain(ks[6], (DEPTH, D_MODEL)),
        "w_in": nrm(ks[7], (DEPTH, D_MODEL, D_IN), D_MODEL ** -0.5),
        "rpb": nrm(ks[8], (DEPTH, NA_HEADS, 2 * NA_KH_MAX - 1, 2 * NA_KW - 1), 0.1),
        "sgu_norm_w": gain(ks[9], (DEPTH, D_SGU)),
        "sgu_w": nrm(ks[10], (DEPTH, SGU_GROUPS, SGU_CHUNK, SGU_CHUNK), SGU_CHUNK ** -0.5),
        "sgu_b": gain(ks[11], (DEPTH, SGU_GROUPS, SGU_CHUNK)),
        "grp_norm_na": gain(ks[12], (DEPTH, D_NA)),
        "grp_norm_sgu": gain(ks[13], (DEPTH, D_SGU)),
        "w_out": nrm(ks[14], (DEPTH, D_MODEL, D_MODEL), D_MODEL ** -0.5),
        "norm2_w": gain(ks[15], (DEPTH, D_MODEL)),
        "w_ff1": nrm(ks[16], (DEPTH, D_MODEL, D_FF), D_MODEL ** -0.5),
        "w_ff2": nrm(ks[17], (DEPTH, D_FF, D_MODEL), D_FF ** -0.5),
        "final_norm_w": gain(ks[18], (D_MODEL,)),
    }


def reference(x, c, ctx, c_ctx, w_ada, b_ada, norm1_w, w_in, rpb, sgu_norm_w, sgu_w, sgu_b,
              grp_norm_na, grp_norm_sgu, w_out, norm2_w, w_ff1, w_ff2, final_norm_w):
    for l in range(DEPTH):
        last = l == DEPTH - 1
        sh1, sc1, g1, sh2, sc2, g2 = adaln(c, w_ada[l], b_ada[l])
        csh1, csc1, cg1, csh2, csc2, cg2 = adaln(c_ctx, w_ada[l], b_ada[l])

        h = modulate(rmsnorm(x, norm1_w[l]), sh1, sc1)
        hc = modulate(rmsnorm(ctx, norm1_w[l]), csh1, csc1)
        q, k, v, u, gt = split_proj(h @ w_in[l])
        if last:
            kc, vc = jnp.split(hc @ w_in[l][:, D_NA:3 * D_NA], 2, axis=-1)
        else:
            qc, kc, vc, uc, gc = split_proj(hc @ w_in[l])
        kc_h, vc_h = to_heads(kc), to_heads(vc)
        o_na = neighbourhood_attention(to_heads(q), to_heads(k), to_heads(v), kc_h, vc_h, rpb[l])
        o_sgu = spatial_gating(jax.nn.gelu(u), jax.nn.gelu(gt), sgu_w[l], sgu_b[l], sgu_norm_w[l])
        x = x + g1 * merge_groups(o_na, o_sgu, grp_norm_na[l], grp_norm_sgu[l], w_out[l])

        h2 = modulate(rmsnorm(x, norm2_w[l]), sh2, sc2)
        x = x + g2 * squared_relu_mlp(h2, w_ff1[l], w_ff2[l])

        if not last:
            oc_na = context_attention(to_heads(qc), kc_h, vc_h)
            oc_sgu = spatial_gating(jax.nn.gelu(uc), jax.nn.gelu(gc), sgu_w[l], sgu_b[l], sgu_norm_w[l])
            ctx = ctx + cg1 * merge_groups(oc_na, oc_sgu, grp_norm_na[l], grp_norm_sgu[l], w_out[l])
            hc2 = modulate(rmsnorm(ctx, norm2_w[l]), csh2, csc2)
            ctx = ctx + cg2 * squared_relu_mlp(hc2, w_ff1[l], w_ff2[l])

    return rmsnorm(x, final_norm_w)
```

```python
import os
import numpy as np
from contextlib import ExitStack
import concourse.bass as bass
import concourse.mybir as mybir
from concourse.bass_utils import run_bass_kernel_spmd

F32 = mybir.dt.float32
BF16 = mybir.dt.bfloat16
AF = mybir.ActivationFunctionType
ALU = mybir.AluOpType
AX = mybir.AxisListType

D = 4096
DNA = 2048
DIN = 10240
DFF = 16384
NH = 16
EPS = 1e-6
NEG = -30000.0
SCALE = 128.0 ** -0.5
DEBUG = bool(int(os.environ.get("MK_DEBUG", "0")))
PH_STOP = int(os.environ.get("MK_STOP", "99"))


class _Op:
    __slots__ = ("eng", "fn", "deps", "signal", "val", "semkey", "is_dma")

    def __init__(self, eng, fn, deps, semkey=None):
        self.eng = eng
        self.fn = fn
        self.deps = deps
        self.signal = False
        self.val = 0
        self.semkey = semkey
        self.is_dma = semkey is not None


class Sched:
    ENGS = ("pe", "act", "dve", "pool", "sp")

    def __init__(self, nc):
        self.nc = nc
        self.q = {e: [] for e in self.ENGS}
        self.res = {}
        self.dma_count = {}
        self.last = {e: None for e in self.ENGS}
        self.lastdma = {}

    def _R(self, name):
        r = self.res.get(name)
        if r is None:
            r = ({}, {})
            self.res[name] = r
        return r

    def _add(self, eng, stream, fn, reads, writes, semkey=None, accum=False):
        deps = []
        for r in reads:
            deps.extend(self._R(r)[0].values())
        for w in writes:
            R = self._R(w)
            if not accum:
                deps.extend(R[0].values())
                deps.extend(R[1].values())
        seen = set()
        dd = []
        for d in deps:
            if id(d) in seen:
                continue
            seen.add(id(d))
            if eng == "pe" and semkey is None and (not d.is_dma) and d.eng == "pe":
                continue
            dd.append(d)
            if not d.is_dma:
                d.signal = True
        op = _Op(eng, fn, dd, semkey)
        if semkey is not None:
            c = self.dma_count.get(semkey, 0) + 16
            self.dma_count[semkey] = c
            op.val = c
            self.lastdma[semkey] = op
        self.q[eng].append(op)
        if semkey is None:
            self.last[eng] = op
        for r in reads:
            self._R(r)[1][stream] = op
        for w in writes:
            R = self._R(w)
            if not accum:
                R[0].clear()
                R[1].clear()
            R[0][stream] = op
        return op

    def op(self, eng, fn, reads=(), writes=()):
        return self._add(eng, eng, fn, reads, writes)

    def dma(self, eng, key, fn, reads=(), writes=(), accum=False):
        return self._add(eng, "dma:" + key, fn, reads, writes, semkey=key, accum=accum)

    def fence(self):
        allops = [o for o in self.last.values() if o is not None]
        allops.extend(self.lastdma.values())
        for o in allops:
            if not o.is_dma:
                o.signal = True
        for e in self.ENGS:
            deps = [o for o in allops if not (e == "pe" and (not o.is_dma) and o.eng == "pe")]
            self.q[e].append(_Op(e, None, deps))
        self.res = {}

    def emit(self, stack):
        nc = self.nc
        esem = {e: stack.enter_context(nc.semaphore("e_" + e)) for e in self.ENGS}
        dsem = {k: stack.enter_context(nc.semaphore("d_" + k)) for k in self.dma_count}
        for e in self.ENGS:
            c = 0
            for o in self.q[e]:
                if (not o.is_dma) and o.signal:
                    c += 1
                    o.val = c
            assert c < 60000, (e, c)
        block = stack.enter_context(nc.Block())
        stats = {}

        def make(ename):
            def body(eng):
                waited = {}
                nw = 0
                for o in self.q[ename]:
                    for d in o.deps:
                        if d.is_dma:
                            key = "d_" + d.semkey
                            sem = dsem[d.semkey]
                        else:
                            key = "e_" + d.eng
                            sem = esem[d.eng]
                        if waited.get(key, 0) >= d.val:
                            continue
                        eng.wait_ge(sem, d.val)
                        waited[key] = d.val
                        nw += 1
                    if o.fn is None:
                        continue
                    ins = o.fn(eng)
                    if o.is_dma:
                        ins.then_inc(dsem[o.semkey], 16)
                    elif o.signal:
                        ins.then_inc(esem[ename], 1)
                stats[ename] = (len(self.q[ename]), nw)
            return body

        block.tensor(make("pe"))
        block.scalar(make("act"))
        block.vector(make("dve"))
        block.gpsimd(make("pool"))
        block.sync(make("sp"))
        return stats


class Arena:
    LO = 16512
    HI = 229344
    _n = [0]

    def __init__(self, nc, start=None):
        self.nc = nc
        self.off = self.LO if start is None else start

    def t(self, name, shape, dt):
        esz = 2 if dt == BF16 else 4
        n = 1
        for s in shape[1:]:
            n *= s
        nbytes = n * esz
        off = (self.off + 63) // 64 * 64
        assert off + nbytes <= self.HI, (name, off, nbytes)
        self._n[0] += 1
        h = self.nc.alloc_sbuf_tensor_at(f"{name}_{self._n[0]}", list(shape), dt, offset=off)
        self.off = off + nbytes
        return h


def _rows(ap2d):
    return ap2d.rearrange("(kc p) c -> p kc c", p=128)


def build_program():
    nc = bass.Bass("TRN2", target_bir_lowering=False)
    din = lambda name, shape, dt=F32: nc.dram_tensor(name, list(shape), dt, kind="ExternalInput").ap()
    skind = "ExternalOutput" if DEBUG else "Internal"
    dscr = lambda name, shape, dt: nc.dram_tensor(name, list(shape), dt, kind=skind).ap()

    xo = din("xo", [2048, D])
    xh = din("xh", [768, D])
    cT = din("cT", [128, 32, 2])
    w_ada = din("w_ada", [D, 6 * D])
    b_ada = din("b_ada", [6 * D])
    norm1_w = din("norm1_w", [D])
    w_in = din("w_in", [D, DIN])
    rpbx = din("rpbx", [NH, 128, 5, 768])
    maskx = din("maskx", [128, 5, 768])
    sgu_nw = din("sgu_nw", [DNA])
    sgu_wT = din("sgu_wT", [128, 4, 128])
    sgu_bT = din("sgu_bT", [128, 4])
    gn_na = din("gn_na", [DNA])
    gn_sgu = din("gn_sgu", [DNA])
    w_out = din("w_out", [D, D])
    norm2_w = din("norm2_w", [D])
    w_ff1 = din("w_ff1", [D, DFF])
    w_ff2 = din("w_ff2", [DFF, D])
    fnw = din("fnw", [D])
    identf = din("identf", [128, 128])
    out = nc.dram_tensor("out", [2048, D], F32, kind="ExternalOutput").ap()

    modbc = dscr("modbc", [8, 128, D], F32)
    qT_s = dscr("qT_s", [NH, 128, 2048], BF16)
    kT_s = dscr("kT_s", [NH, 128, 2816], BF16)
    v_s = dscr("v_s", [2816, DNA], BF16)
    gu_s = dscr("gu_s", [2048, DNA], F32)
    gg_s = dscr("gg_s", [2048, DNA], F32)
    ona_s = dscr("ona_s", [2048, DNA], F32)
    x1_s = dscr("x1_s", [2048, D], F32)
    hid_s = dscr("hid_s", [DFF, 2048], BF16)
    x2_s = dscr("x2_s", [2048, D], F32)

    with ExitStack() as st:
        S = Sched(nc)
        PS = [nc.alloc_psum_tensor(f"ps{i}", [128, 1024], F32) for i in range(4)]

        def bank(i):
            return PS[i // 2][:, (i % 2) * 512:(i % 2) * 512 + 512]

        def bank_bf(i):
            return PS[i // 2][:, (i % 2) * 512:(i % 2) * 512 + 512].bitcast(BF16)

        A0 = Arena(nc)
        ident = A0.t("ident", [128, 128], BF16)
        identf_sb = A0.t("identf", [128, 128], F32)
        stat1 = A0.t("stat1", [128, 16, 8], F32)
        stat2 = A0.t("stat2", [128, 16, 4], F32)
        rstd2 = A0.t("rstd2", [128, 16], F32)
        rstd3 = A0.t("rstd3", [128, 16], F32)
        PBASE = A0.off

        S.dma("sp", "identf", lambda e: e.dma_start(out=identf_sb[:], in_=identf), writes=["identf"])
        S.op("dve", lambda e: e.tensor_copy(out=ident[:], in_=identf_sb[:]), reads=["identf"], writes=["ident"])

        def rstd_from_ss(ss_ap, out_ap, n, rs, ws):
            S.op("act", lambda e: e.activation(out=out_ap, in_=ss_ap, func=AF.Sqrt, bias=EPS, scale=1.0 / n),
                 reads=rs, writes=ws)
            S.op("dve", lambda e: e.reciprocal(out=out_ap, in_=out_ap), reads=ws, writes=ws)

        def transpose_tile(src, srcres, dstT, dstres, tok0, nkc, tpb):
            for g in range(nkc // 8):
                b = tpb[g % 2]
                pb = bank_bf(b)
                for j in range(8):
                    kc = g * 8 + j
                    S.op("pe", (lambda kc, j, pb, src: lambda e: e.transpose(out=pb[:, j * 128:(j + 1) * 128], in_=src[:, kc * 128:(kc + 1) * 128], identity=ident[:]))(kc, j, pb, src),
                         reads=[(srcres(g) if callable(srcres) else srcres), "ident"], writes=[f"bank{b}"])
                eng = "act" if g % 2 == 0 else "dve"
                dst = dstT[:, g * 8:(g + 1) * 8, tok0:tok0 + 128]
                srcv = pb.rearrange("p (j t) -> p j t", j=8)
                if eng == "act":
                    S.op("act", (lambda dst, srcv: lambda e: e.copy(out=dst, in_=srcv))(dst, srcv), reads=[f"bank{b}"], writes=[dstres])
                else:
                    S.op("dve", (lambda dst, srcv: lambda e: e.tensor_copy(out=dst, in_=srcv))(dst, srcv), reads=[f"bank{b}"], writes=[dstres])

        def load_w(slot, slotres, key, wview, c0, nkc=32, k0=0, ncol=512):
            S.dma("pool", key, (lambda slot, wview, c0: lambda e: e.dma_start(out=slot[:, 0:nkc, 0:ncol], in_=wview[:, k0:k0 + nkc, c0:c0 + ncol]))(slot, wview, c0),
                  writes=[slotres])

        def bcast_load(dst, dstres, key, vec_ap):
            S.dma("sp", key, lambda e: e.dma_start(out=dst, in_=vec_ap.partition_broadcast(128)), writes=[dstres])

        def make_ada(AO, pfx, jlist, pbank):
            wE_ = [AO.t(f"{pfx}w{i}", [128, 8, 512], BF16) for i in range(2)]
            lhs_ = AO.t(f"{pfx}lhs", [128, 32, 128], BF16)
            bE_ = AO.t(f"{pfx}b", [128, 512], F32)
            nE_ = AO.t(f"{pfx}n", [128, 512], F32)
            eE_ = AO.t(f"{pfx}e", [128, 512], F32)
            cT_ = AO.t(f"{pfx}c", [128, 32, 2], F32)
            nq = 4 * len(jlist)
            wv = _rows(w_ada)

            def load(q, qend):
                j = jlist[q // 4]
                qt = q % 4
                sl = q % 2
                S.dma("pool", f"{pfx}w{sl}", lambda e: e.dma_start(out=wE_[sl][:], in_=wv[:, qt * 8:(qt + 1) * 8, j * 512:(j + 1) * 512]), writes=[f"{pfx}w{sl}"])

            def prep(q0, qend):
                S.dma("sp", f"{pfx}c", lambda e: e.dma_start(out=cT_[:], in_=cT), writes=[f"{pfx}c"])
                S.op("act", lambda e: e.activation(out=cT_[:], in_=cT_[:], func=AF.Silu), reads=[f"{pfx}c"], writes=[f"{pfx}c"])
                S.op("dve", lambda e: e.tensor_copy(out=lhs_[:], in_=cT_[:, :, 0:1].to_broadcast([128, 32, 128])), reads=[f"{pfx}c"], writes=[f"{pfx}lhs"])
                for q in range(q0, min(q0 + 2, qend)):
                    load(q, qend)

            def quarter(q, qend):
                j = jlist[q // 4]
                qt = q % 4
                which = j // 8
                cb = j % 8
                sl = q % 2
                pb = bank(pbank)
                if qt == 0:
                    bcast_load(bE_[:], f"{pfx}b", f"{pfx}b", b_ada[j * 512:(j + 1) * 512])
                    if which == 4:
                        bcast_load(nE_[:], f"{pfx}n", f"{pfx}n", norm2_w[cb * 512:(cb + 1) * 512])
                for k in range(8):
                    kc = qt * 8 + k
                    S.op("pe", (lambda k, kc: lambda e: e.matmul(pb, lhsT=lhs_[:, kc, :], rhs=wE_[sl][:, k, :], start=(kc == 0), stop=(kc == 31)))(k, kc),
                         reads=[f"{pfx}w{sl}", f"{pfx}lhs"], writes=[f"bank{pbank}"])
                if q + 2 < qend:
                    load(q + 2, qend)
                if qt == 3:
                    S.op("dve", lambda e: e.tensor_tensor(out=eE_[:], in0=pb, in1=bE_[:], op=ALU.add), reads=[f"bank{pbank}", f"{pfx}b"], writes=[f"{pfx}e"])
                    if which == 4:
                        S.op("dve", lambda e: e.scalar_tensor_tensor(out=eE_[:], in0=eE_[:], scalar=1.0, in1=nE_[:], op0=ALU.add, op1=ALU.mult),
                             reads=[f"{pfx}e", f"{pfx}n"], writes=[f"{pfx}e"])
                    S.dma("sp", f"{pfx}e", lambda e: e.dma_start(out=modbc[which, :, cb * 512:(cb + 1) * 512], in_=eE_[:]), reads=[f"{pfx}e"], writes=["modbc"], accum=True)
            return prep, quarter, nq

        A = Arena(nc, PBASE)
        cTs = A.t("cTs", [128, 32, 2], F32)
        lhs_lat = A.t("lhs_lat", [128, 32, 128], BF16)
        lhs_ctx = A.t("lhs_ctx", [128, 32, 128], BF16)
        wA_ = [A.t(f"wA_{i}", [128, 32, 512], BF16) for i in range(3)]
        bA_ = [A.t(f"bA_{i}", [128, 512], F32) for i in range(2)]
        nA_ = [A.t(f"nA_{i}", [128, 512], F32) for i in range(2)]
        eA_ = [A.t(f"ev{i}", [128, 512], F32) for i in range(4)]
        S.dma("sp", "cTs", lambda e: e.dma_start(out=cTs[:], in_=cT), writes=["cTs"])
        S.op("act", lambda e: e.activation(out=cTs[:], in_=cTs[:], func=AF.Silu), reads=["cTs"], writes=["cTs"])
        S.op("dve", lambda e: e.tensor_copy(out=lhs_lat[:], in_=cTs[:, :, 0:1].to_broadcast([128, 32, 128])), reads=["cTs"], writes=["lhs_lat"])
        S.op("dve", lambda e: e.tensor_copy(out=lhs_ctx[:], in_=cTs[:, :, 1:2].to_broadcast([128, 32, 128])), reads=["cTs"], writes=["lhs_ctx"])
        wada_v = _rows(w_ada)
        KS = 19
        fst = [A.t(f"fst{i}", [128, 32 - KS, 512], F32) for i in range(2)]
        dest_lat = {0: 0, 1: 1, 2: 2, 3: 3, 4: 4, 5: 5}
        evi = 0
        NB0 = 16

        def fst_load(j):
            f_ = fst[j % 2]
            S.dma("act", f"fst{j % 2}", lambda e: e.dma_start(out=f_[:], in_=wada_v[:, KS:32, j * 512:(j + 1) * 512]), writes=[f"fst{j % 2}"])

        for j in range(NB0):
            which = j // 8
            cb = j % 8
            sl = j % 3
            ra, rb = f"wA_{sl}a", f"wA_{sl}b"
            S.dma("pool", f"wA_{sl}", (lambda sl, j: lambda e: e.dma_start(out=wA_[sl][:, 0:KS, :], in_=wada_v[:, 0:KS, j * 512:(j + 1) * 512]))(sl, j), writes=[ra])
            f_ = fst[j % 2]
            fr = f"fst{j % 2}"
            if j == 0:
                fst_load(0)
            if j + 1 < NB0:
                fst_load(j + 1)
            S.op("act", (lambda f_, sl: lambda e: e.copy(out=wA_[sl][:, KS:32, :], in_=f_[:]))(f_, sl), reads=[fr], writes=[rb])
            bsl = j % 2
            bcast_load(bA_[bsl][:], f"bA_{bsl}", f"bA_{bsl}", b_ada[j * 512:(j + 1) * 512])
            is_sc = which in (1, 4)
            if is_sc:
                nv = norm1_w if which == 1 else norm2_w
                bcast_load(nA_[bsl][:], f"nA_{bsl}", f"nA_{bsl}", nv[cb * 512:(cb + 1) * 512])
            variants = [("lat", lhs_lat, dest_lat[which])]
            if which < 2:
                variants.append(("ctx", lhs_ctx, 6 + which))
            for (vn, lhs, dst_i) in variants:
                b = evi % 4
                pb = bank(b)
                for kc in range(32):
                    S.op("pe", (lambda pb, lhs, sl, kc: lambda e: e.matmul(pb, lhsT=lhs[:, kc, :], rhs=wA_[sl][:, kc, :], start=(kc == 0), stop=(kc == 31)))(pb, lhs, sl, kc),
                         reads=[(ra if kc < KS else rb), "lhs_" + vn], writes=[f"bank{b}"])
                e_t = eA_[b]
                S.op("dve", (lambda e_t, pb, bsl: lambda e: e.tensor_tensor(out=e_t[:], in0=pb, in1=bA_[bsl][:], op=ALU.add))(e_t, pb, bsl),
                     reads=[f"bank{b}", f"bA_{bsl}"], writes=[f"ev{b}"])
                if is_sc:
                    S.op("dve", (lambda e_t, bsl: lambda e: e.scalar_tensor_tensor(out=e_t[:], in0=e_t[:], scalar=1.0, in1=nA_[bsl][:], op0=ALU.add, op1=ALU.mult))(e_t, bsl),
                         reads=[f"ev{b}", f"nA_{bsl}"], writes=[f"ev{b}"])
                S.dma("sp", f"ev{b}", (lambda e_t, dst_i, cb: lambda e: e.dma_start(out=modbc[dst_i, :, cb * 512:(cb + 1) * 512], in_=e_t[:]))(e_t, dst_i, cb),
                      reads=[f"ev{b}"], writes=["modbc"], accum=True)
                evi += 1
        S.fence()
        if PH_STOP <= 0:
            return finish(nc, S, st, out)

        A = Arena(nc, PBASE)
        hT = A.t("hT", [128, 32, 1024], BF16)
        OV1 = A.off
        gwb = A.t("gwb", [128, D], F32)
        shb = A.t("shb", [128, D], F32)
        AO1 = Arena(nc, OV1)
        ada1_prep, ada1_quarter, ada1_nq = make_ada(AO1, "a1", list(range(24, 40)), 4)
        assert AO1.off <= A.off
        wsl = [A.t(f"w1sl{i}", [128, 32, 512], BF16) for i in range(2)]
        UB = A.off
        xs = [A.t(f"xs{i}", [128, D], F32) for i in range(2)]
        hb = A.t("hb", [128, D], BF16)
        ss = A.t("ss", [128, 8], F32)
        junkp = A.t("junkp", [128, 1024], BF16)
        A2 = Arena(nc, UB)
        stq = [A2.t(f"stq{i}", [128, 4, 1024], BF16) for i in range(2)]
        stv = [A2.t(f"stv{i}", [128, 8, 512], BF16) for i in range(2)]
        stg = [A2.t(f"stg{i}", [128, 512], F32) for i in range(4)]
        win_v = _rows(w_in)

        mod_loaded = {}

        def prologue(src_ap, nsub, sub_mod, hT_t, gw_t, sh_t, xs, hb, ss, rstd_ext=None, junk=None, force_reload=False):
            state = {"cur": None if force_reload else mod_loaded.get(id(gw_t))}
            HL = 1024

            def a1(s):
                m = sub_mod(s)
                mod_loaded[id(gw_t)] = m
                if m != state["cur"]:
                    S.dma("sp", "gwb", lambda e: e.dma_start(out=gw_t[:], in_=modbc[m[0]]), reads=["modbc"], writes=["gwb"])
                    S.dma("sp", "shb", lambda e: e.dma_start(out=sh_t[:], in_=modbc[m[1]]), reads=["modbc"], writes=["shb"])
                    state["cur"] = m
                x_t = xs[s % 2]
                xr = f"xs{s % 2}"
                S.dma("sp", xr, lambda e: e.dma_start(out=x_t[:], in_=src_ap[s * 128:(s + 1) * 128, :]), writes=[xr])
                if rstd_ext is None:
                    for c4 in range(4):
                        S.op("act", (lambda c4: lambda e: e.activation(out=junk[:], in_=x_t[:, c4 * 1024:(c4 + 1) * 1024], func=AF.Square, accum_out=ss[:, c4:c4 + 1]))(c4),
                             reads=[xr], writes=["junkp", "ss4"])
                    S.op("dve", lambda e: e.reduce_sum(out=ss[:, 4:5], in_=ss[:, 0:4], axis=AX.X), reads=["ss4"], writes=["ss"])
                    rstd_from_ss(ss[:, 4:5], ss[:, 5:6], D, ["ss"], ["ss1"])
                    rs_ap = ss[:, 5:6]
                    rs_res = "ss1"
                else:
                    rs_ap = rstd_ext(s)
                    rs_res = "rstd_ext"
                S.op("dve", lambda e: e.scalar_tensor_tensor(out=x_t[:], in0=x_t[:], scalar=rs_ap, in1=gw_t[:], op0=ALU.mult, op1=ALU.mult),
                     reads=[xr, rs_res, "gwb"], writes=[xr])

            def a2(s):
                x_t = xs[s % 2]
                xr = f"xs{s % 2}"
                S.op("pool", lambda e: e.tensor_tensor(out=hb[:, 0:HL], in0=x_t[:, 0:HL], in1=sh_t[:, 0:HL], op=ALU.add), reads=[xr, "shb"], writes=["hb_lo"])
                S.op("dve", lambda e: e.tensor_tensor(out=hb[:, HL:D], in0=x_t[:, HL:D], in1=sh_t[:, HL:D], op=ALU.add), reads=[xr, "shb"], writes=["hb_hi"])

            def b_(s):
                transpose_tile(hb, (lambda g: "hb_lo" if g < 1 else "hb_hi"), hT_t, "hT", s * 128, 32, (6, 7))

            for s in range(nsub + 1):
                if s < nsub:
                    a1(s)
                if s >= 1:
                    b_(s - 1)
                if s < nsub:
                    a2(s)

        def evac(eng, dst, src, reads, writes, func=None):
            if eng == "act":
                if func is None:
                    S.op("act", lambda e: e.copy(out=dst, in_=src), reads=reads, writes=writes)
                else:
                    S.op("act", lambda e: e.activation(out=dst, in_=src, func=func), reads=reads, writes=writes)
            else:
                S.op("dve", lambda e: e.tensor_copy(out=dst, in_=src), reads=reads, writes=writes)

        def p1_pass(pname, src_ap, nsub, sub_mod, pieces, tokblocks, kcol0, qcol0, vrows, urow0, adaq=None):
            colof = lambda kind, idx: {"q": 0, "k": 2048, "v": 4096, "u": 6144, "g": 8192}[kind] + idx * 512
            for p_ in range(2):
                load_w(wsl[p_], f"w1sl{p_}", f"w1sl{p_}", win_v, colof(*pieces[p_]))
            prologue(src_ap, nsub, sub_mod, hT, gwb, shb, xs, hb, ss, junk=junkp, force_reload=True)
            S.fence()
            if adaq is not None:
                ada1_prep(adaq[0], adaq[1])
                adan = adaq[0]
            accb = 0
            nq = 0
            nv = 0
            ng = 0
            npieces = len(pieces)
            for pi, (kind, idx) in enumerate(pieces):
                sl = pi % 2
                col0 = {"q": 0, "k": 2048, "v": 4096, "u": 6144, "g": 8192}[kind] + idx * 512
                W = wsl[sl]
                wres = f"w1sl{sl}"
                if kind in ("q", "k"):
                    sq = stq[nq % 2]
                    sqr = f"stq{nq % 2}"
                    nq += 1
                    for hh in range(4):
                        for (t0, n) in tokblocks:
                            b = accb % 4
                            accb += 1
                            pb = bank(b)
                            for kc in range(32):
                                S.op("pe", (lambda pb, W, kc, hh, t0, n, hT=hT: lambda e: e.matmul(pb[:, 0:n], lhsT=W[:, kc, hh * 128:(hh + 1) * 128], rhs=hT[:, kc, t0:t0 + n], start=(kc == 0), stop=(kc == 31)))(pb, W, kc, hh, t0, n),
                                     reads=[wres, "hT"], writes=[f"bank{b}"])
                            evac("act" if (accb % 2) else "dve", sq[:, hh, t0:t0 + n], pb[:, 0:n], [f"bank{b}"], [sqr])
                    h0 = idx * 4
                    dst_s = qT_s if kind == "q" else kT_s
                    cols = qcol0 if kind == "q" else kcol0
                    for bi, (t0, n) in enumerate(tokblocks):
                        for (dcol, tt0, nn) in cols[bi]:
                            S.dma("sp", sqr, (lambda sq, dst_s, h0, dcol, tt0, nn: lambda e: e.dma_start(out=dst_s[h0:h0 + 4, :, dcol:dcol + nn].rearrange("h d t -> d h t"), in_=sq[:, :, tt0:tt0 + nn]))(sq, dst_s, h0, dcol, tt0, nn),
                                  reads=[sqr], writes=[("qT_s" if kind == "q" else "kT_s")], accum=True)
                elif kind == "v":
                    sv = stv[nv % 2]
                    svr = f"stv{nv % 2}"
                    nv += 1
                    for s in range(nsub):
                        b = accb % 4
                        accb += 1
                        pb = bank(b)
                        for kc in range(32):
                            S.op("pe", (lambda pb, W, kc, s, hT=hT: lambda e: e.matmul(pb, lhsT=hT[:, kc, s * 128:(s + 1) * 128], rhs=W[:, kc, :], start=(kc == 0), stop=(kc == 31)))(pb, W, kc, s),
                                 reads=[wres, "hT"], writes=[f"bank{b}"])
                        evac("act" if (accb % 2) else "dve", sv[:, s, :], pb, [f"bank{b}"], [svr])
                    for (row0, s0, ns) in vrows:
                        S.dma("sp", svr, (lambda sv, row0, s0, ns, idx: lambda e: e.dma_start(out=v_s[row0:row0 + ns * 128, idx * 512:(idx + 1) * 512].rearrange("(s p) c -> p s c", p=128), in_=sv[:, s0:s0 + ns, :]))(sv, row0, s0, ns, idx),
                              reads=[svr], writes=["v_s"], accum=True)
                else:
                    dst_s = gu_s if kind == "u" else gg_s
                    for s in range(nsub):
                        b = accb % 4
                        accb += 1
                        pb = bank(b)
                        for kc in range(32):
                            S.op("pe", (lambda pb, W, kc, s, hT=hT: lambda e: e.matmul(pb, lhsT=hT[:, kc, s * 128:(s + 1) * 128], rhs=W[:, kc, :], start=(kc == 0), stop=(kc == 31)))(pb, W, kc, s),
                                 reads=[wres, "hT"], writes=[f"bank{b}"])
                        sg = stg[ng % 4]
                        sgr = f"stg{ng % 4}"
                        ng += 1
                        S.op("act", (lambda sg, pb: lambda e: e.activation(out=sg[:], in_=pb, func=AF.Gelu_apprx_tanh))(sg, pb),
                             reads=[f"bank{b}"], writes=[sgr])
                        row0 = urow0 + s * 128
                        S.dma("sp", sgr, (lambda sg, dst_s, row0, idx: lambda e: e.dma_start(out=dst_s[row0:row0 + 128, idx * 512:(idx + 1) * 512], in_=sg[:]))(sg, dst_s, row0, idx),
                              reads=[sgr], writes=[("gu_s" if kind == "u" else "gg_s")], accum=True)
                if pi + 2 < npieces:
                    load_w(wsl[sl], f"w1sl{sl}", f"w1sl{sl}", win_v, colof(*pieces[pi + 2]))
                if adaq is not None:
                    tgt = adaq[0] + ((adaq[1] - adaq[0]) * (pi + 1) + npieces - 1) // npieces
                    while adan < min(tgt, adaq[1]):
                        ada1_quarter(adan, adaq[1])
                        adan += 1
            S.fence()

        piecesA = [("k", i) for i in range(4)] + [("v", i) for i in range(4)]
        p1_pass("A", xh, 6, lambda s: (7, 6) if s >= 4 else (1, 0), piecesA,
                tokblocks=[(0, 512), (512, 256)],
                kcol0=[[(0, 0, 256), (2304, 256, 256)], [(2560, 512, 256)]], qcol0=None,
                vrows=[(0, 0, 2), (2304, 2, 2), (2560, 4, 2)], urow0=None)
        piecesO = [(k, i) for k in ("q", "k", "v", "g", "u") for i in range(4)]
        for half in range(2):
            t00 = half * 1024
            p1_pass("BC"[half], xo[t00:t00 + 1024, :], 8, lambda s: (1, 0), piecesO,
                    tokblocks=[(0, 512), (512, 512)],
                    kcol0=[[(256 + t00, 0, 512)], [(256 + t00 + 512, 512, 512)]],
                    qcol0=[[(t00, 0, 512)], [(t00 + 512, 512, 512)]],
                    vrows=[(256 + t00, 0, 8)], urow0=t00, adaq=(half * 32, half * 32 + 32))
        if PH_STOP <= 1:
            return finish(nc, S, st, out)

        A = Arena(nc, PBASE)
        maskt = A.t("maskt", [128, 5, 768], F32)
        G2 = [dict(q=A.t(f"aq{i}", [128, 2, 2048], BF16), k=A.t(f"ak{i}", [128, 2, 2816], BF16),
                   v=A.t(f"av{i}", [128, 22, 256], BF16)) for i in range(2)]
        tabs2 = [A.t(f"tab{i}", [128, 5, 768], F32) for i in range(2)]
        Sb2 = [A.t(f"Sb{i}", [128, 1024], F32) for i in range(2)]
        Pb2 = [A.t(f"Pb{i}", [128, 1024], BF16) for i in range(2)]
        PT2 = [A.t(f"PT{i}", [128, 1024], BF16) for i in range(2)]
        st2 = [A.t(f"ast{i}", [128, 8], F32) for i in range(4)]
        ost2 = [A.t(f"ost{i}", [128, 16, 128], F32) for i in range(2)]
        cTs2 = A.t("cTs2", [128, 32, 2], F32)
        lhsE = A.t("lhsE", [128, 32, 128], BF16)
        wE = [A.t(f"wE{i}", [128, 16, 512], BF16) for i in range(2)]
        bE = [A.t(f"bE{i}", [128, 512], F32) for i in range(2)]
        nE = [A.t(f"nE{i}", [128, 512], F32) for i in range(2)]
        eE = [A.t(f"eE{i}", [128, 512], F32) for i in range(2)]
        S.dma("sp", "maskt", lambda e: e.dma_start(out=maskt[:], in_=maskx), writes=["maskt"])
        S.dma("sp", "cTs2", lambda e: e.dma_start(out=cTs2[:], in_=cT), writes=["cTs2"])
        S.op("act", lambda e: e.activation(out=cTs2[:], in_=cTs2[:], func=AF.Silu), reads=["cTs2"], writes=["cTs2"])
        S.op("dve", lambda e: e.tensor_copy(out=lhsE[:], in_=cTs2[:, :, 0:1].to_broadcast([128, 32, 128])), reads=["cTs2"], writes=["lhsE"])

        def ada_load(hp):
            j = 16 + hp // 2
            hf = hp % 2
            sl = hp % 2
            S.dma("pool", f"wE{sl}", lambda e: e.dma_start(out=wE[sl][:], in_=wada_v[:, hf * 16:(hf + 1) * 16, j * 512:(j + 1) * 512]), writes=[f"wE{sl}"])

        def ada_block(n):
            j = 16 + n
            which = j // 8
            cb = j % 8
            bsl = n % 2
            bcast_load(bE[bsl][:], f"bE{bsl}", f"bE{bsl}", b_ada[j * 512:(j + 1) * 512])
            if which == 4:
                bcast_load(nE[bsl][:], f"nE{bsl}", f"nE{bsl}", norm2_w[cb * 512:(cb + 1) * 512])
            pb = bank(7)
            for hf in range(2):
                hp = 2 * n + hf
                sl = hp % 2
                for k in range(16):
                    kc = hf * 16 + k
                    S.op("pe", (lambda sl, k, kc: lambda e: e.matmul(pb, lhsT=lhsE[:, kc, :], rhs=wE[sl][:, k, :], start=(kc == 0), stop=(kc == 31)))(sl, k, kc),
                         reads=[f"wE{sl}", "lhsE"], writes=["bank7"])
                if hp + 2 < 16:
                    ada_load(hp + 2)
            e_t = eE[bsl]
            S.op("dve", lambda e: e.tensor_tensor(out=e_t[:], in0=pb, in1=bE[bsl][:], op=ALU.add), reads=["bank7", f"bE{bsl}"], writes=[f"eE{bsl}"])
            if which == 4:
                S.op("dve", lambda e: e.scalar_tensor_tensor(out=e_t[:], in0=e_t[:], scalar=1.0, in1=nE[bsl][:], op0=ALU.add, op1=ALU.mult),
                     reads=[f"eE{bsl}", f"nE{bsl}"], writes=[f"eE{bsl}"])
            S.dma("sp", f"eE{bsl}", lambda e: e.dma_start(out=modbc[which, :, cb * 512:(cb + 1) * 512], in_=e_t[:]), reads=[f"eE{bsl}"], writes=["modbc"], accum=True)

        for hp in range(2):
            ada_load(hp)

        def pair_cfg(i):
            if i == 0:
                return 0, 0, 12
            if i == 1:
                return 1, 0, 12
            if i == 14:
                return 3, 28, 10
            if i == 15:
                return 4, 28, 12
            return 2, 2 * i, 10

        def attn_stages(t, G, gres, hh, tab, tabres, i, ost, ostres):
            ti, B, nr = pair_cfg(i)
            W = nr * 64
            rem = W - 512
            NC_ = W + 256
            nch = NC_ // 128
            vts = [B // 2 + c for c in range(W // 128)] + [20, 21]
            p2_ = t % 2
            p4_ = t % 4
            Sps = PS[p2_]
            Sr = f"S{p2_}"
            q_ap = G["q"][:, hh, i * 128:(i + 1) * 128]
            kk = G["k"]
            vv = G["v"]
            sb = Sb2[p2_]
            stt = st2[p4_]
            pb_ = Pb2[p2_]
            ptb = bank_bf(4 + p2_)
            pt_ = PT2[p2_]
            ob = bank(6)[:, p2_ * 128:(p2_ + 1) * 128]

            def s1():
                S.op("pe", lambda e: e.matmul(Sps[:, 0:512], lhsT=q_ap, rhs=kk[:, hh, B * 64:B * 64 + 512], start=True, stop=True),
                     reads=[gres + "q", gres + "k"], writes=[Sr])
                S.op("pe", lambda e: e.matmul(Sps[:, 512:512 + rem], lhsT=q_ap, rhs=kk[:, hh, (B + 8) * 64:(B + 8) * 64 + rem], start=True, stop=True),
                     reads=[gres + "q", gres + "k"], writes=[Sr])
                S.op("pe", lambda e: e.matmul(Sps[:, W:W + 256], lhsT=q_ap, rhs=kk[:, hh, 2560:2816], start=True, stop=True),
                     reads=[gres + "q", gres + "k"], writes=[Sr])
                S.op("dve", lambda e: e.scalar_tensor_tensor(out=sb[:, 0:W], in0=Sps[:, 0:W], scalar=SCALE, in1=tab[:, ti, 0:W], op0=ALU.mult, op1=ALU.add),
                     reads=[Sr, tabres], writes=[f"Sb{p2_}"])
                S.op("dve", lambda e: e.tensor_scalar_mul(out=sb[:, W:NC_], in0=Sps[:, W:NC_], scalar1=SCALE), reads=[Sr], writes=[f"Sbc{p2_}"])
                S.op("dve", lambda e: e.reduce_max(out=stt[:, 1:2], in_=sb[:, 0:NC_], axis=AX.X, negate=True), reads=[f"Sb{p2_}", f"Sbc{p2_}"], writes=[f"nmx{p4_}"])
                S.op("act", lambda e: e.activation(out=pb_[:, 0:NC_], in_=sb[:, 0:NC_], func=AF.Exp, bias=stt[:, 1:2], scale=1.0, accum_out=stt[:, 2:3]),
                     reads=[f"Sb{p2_}", f"Sbc{p2_}", f"nmx{p4_}"], writes=[f"Pb{p2_}", f"rs{p4_}"])

            def s2():
                for ci in range(nch):
                    S.op("pe", (lambda ci: lambda e: e.transpose(out=ptb[:, ci * 128:(ci + 1) * 128], in_=pb_[:, ci * 128:(ci + 1) * 128], identity=ident[:]))(ci),
                         reads=[f"Pb{p2_}", "ident"], writes=[f"bank{4 + p2_}"])
                S.op("act", lambda e: e.copy(out=pt_[:, 0:NC_], in_=ptb[:, 0:NC_]), reads=[f"bank{4 + p2_}"], writes=[f"PT{p2_}"])
                S.op("dve", lambda e: e.reciprocal(out=stt[:, 3:4], in_=stt[:, 2:3]), reads=[f"rs{p4_}"], writes=[f"rinv{p4_}"])

            def s3():
                for ci in range(nch):
                    S.op("pe", (lambda ci: lambda e: e.matmul(ob, lhsT=pt_[:, ci * 128:(ci + 1) * 128], rhs=vv[:, vts[ci], hh * 128:(hh + 1) * 128], start=(ci == 0), stop=(ci == nch - 1)))(ci),
                         reads=[f"PT{p2_}", gres + "v"], writes=["bank6"])
                S.op("act", lambda e: e.mul(out=ost[:, i, :], in_=ob, mul=stt[:, 3:4]), reads=["bank6", f"rinv{p4_}"], writes=[ostres])
            return s1, s2, s3

        def p2_load_group(gi):
            G = G2[gi % 2]
            gres = f"G{gi % 2}"
            S.dma("sp", gres + "q", lambda e: e.dma_start(out=G["q"][:], in_=qT_s[2 * gi:2 * gi + 2].rearrange("h d t -> d h t")), reads=["qT_s"], writes=[gres + "q"])
            S.dma("sp", gres + "k", lambda e: e.dma_start(out=G["k"][:], in_=kT_s[2 * gi:2 * gi + 2].rearrange("h d t -> d h t")), reads=["kT_s"], writes=[gres + "k"])
            S.dma("sp", gres + "v", lambda e: e.dma_start(out=G["v"][:], in_=v_s[:, gi * 256:(gi + 1) * 256].rearrange("(t p) c -> p t c", p=128)), reads=["v_s"], writes=[gres + "v"])

        def p2_load_tab(h):
            tab = tabs2[h % 2]
            tr = f"tab{h % 2}"
            S.dma("sp", tr, lambda e: e.dma_start(out=tab[:], in_=rpbx[h]), writes=[tr])
            S.op("pool", lambda e: e.tensor_tensor(out=tab[:], in0=tab[:], in1=maskt[:], op=ALU.add), reads=[tr, "maskt"], writes=[tr])

        iters = [(h // 2, h % 2, h, i) for h in range(NH) for i in range(16)]
        NIT = len(iters)
        stages = {}
        p2_load_group(0)
        p2_load_tab(0)
        nada = 0
        for t in range(NIT + 2):
            if t < NIT:
                gi, hh, h, i = iters[t]
                if i == 0:
                    if h + 1 < NH:
                        p2_load_tab(h + 1)
                if i == 2 and hh == 0 and gi + 1 < 8:
                    p2_load_group(gi + 1)
                stages[t] = attn_stages(t, G2[gi % 2], f"G{gi % 2}", hh, tabs2[h % 2], f"tab{h % 2}", i, ost2[h % 2], f"ost{h % 2}")
                stages[t][0]()
            if 0 <= t - 1 < NIT:
                stages[t - 1][1]()
            if 0 <= t - 2 < NIT:
                stages[t - 2][2]()
                gi_, hh_, h_, i_ = iters[t - 2]
                if i_ == 15:
                    S.dma("sp", f"ost{h_ % 2}", (lambda ost, h_: lambda e: e.dma_start(out=ona_s[:, h_ * 128:(h_ + 1) * 128].rearrange("(i p) c -> p i c", p=128), in_=ost[:]))(ost2[h_ % 2], h_),
                          reads=[f"ost{h_ % 2}"], writes=["ona_s"], accum=True)
                del stages[t - 2]
            if t % 26 == 6 and nada < 8:
                ada_block(nada)
                nada += 1
        while nada < 8:
            ada_block(nada)
            nada += 1
        S.fence()
        if PH_STOP <= 2:
            return finish(nc, S, st, out)

        A = Arena(nc, PBASE)
        ocT = A.t("ocT", [128, 32, 1024], BF16)
        wC = [A.t(f"wC{i}", [128, 32, 512], BF16) for i in range(2)]
        cn_sgu = A.t("cn_sgu", [128, DNA], F32)
        cn_na = A.t("cn_na", [128, DNA], F32)
        cn_gs = A.t("cn_gs", [128, DNA], F32)
        wsT_f = A.t("wsT_f", [128, 4, 128], F32)
        wsT = A.t("wsT", [128, 4, 128], BF16)
        sbT = A.t("sbT", [128, 4], F32)
        st4 = A.t("st4", [128, 8], F32)
        UB4 = A.off
        ona_t = A.t("ona_t", [128, DNA], F32)
        gu_t = A.t("gu_t", [128, DNA], F32)
        gg_t = A.t("gg_t", [128, DNA], F32)
        oc_tok = A.t("oc_tok", [128, D], BF16)
        gn_t = A.t("gn_t", [128, DNA], BF16)
        junk4 = A.t("junk4", [128, DNA], BF16)
        A2 = Arena(nc, UB4)
        xb4 = [A2.t(f"xb4_{i}", [128, 512], F32) for i in range(3)]
        tt4 = [A2.t(f"tt4_{i}", [128, 512], F32) for i in range(2)]
        g1b = [A2.t(f"g1b{i}", [128, 512], F32) for i in range(2)]
        junk4b = A2.t("junk4b", [128, 512], BF16)
        bcast_load(cn_sgu[:], "cn_sgu", "cn_sgu", sgu_nw)
        bcast_load(cn_na[:], "cn_na", "cn_na", gn_na)
        bcast_load(cn_gs[:], "cn_gs", "cn_gs", gn_sgu)
        S.dma("sp", "wsT_f", lambda e: e.dma_start(out=wsT_f[:], in_=sgu_wT), writes=["wsT_f"])
        S.dma("sp", "sbT", lambda e: e.dma_start(out=sbT[:], in_=sgu_bT), writes=["sbT"])
        S.op("dve", lambda e: e.tensor_copy(out=wsT[:], in_=wsT_f[:]), reads=["wsT_f"], writes=["wsT"])
        wout_v = _rows(w_out)

        def sq_rstd(src, srcres, junk, junkres, col, n):
            S.op("act", lambda e: e.activation(out=junk, in_=src, func=AF.Square, accum_out=st4[:, col:col + 1]), reads=[srcres], writes=[junkres, f"st4_{col}"])
            rstd_from_ss(st4[:, col:col + 1], st4[:, col + 1:col + 2], n, [f"st4_{col}"], [f"st4_{col + 1}"])

        def p4_prologue(t00):
            def ld(which, s):
                r0_ = t00 + s * 128
                if which == "gg":
                    S.dma("sp", "gg_t", lambda e: e.dma_start(out=gg_t[:], in_=gg_s[r0_:r0_ + 128, :]), reads=["gg_s"], writes=["gg_t"])
                elif which == "gu":
                    S.dma("sp", "gu_t", lambda e: e.dma_start(out=gu_t[:], in_=gu_s[r0_:r0_ + 128, :]), reads=["gu_s"], writes=["gu_t"])
                else:
                    S.dma("sp", "ona_t", lambda e: e.dma_start(out=ona_t[:], in_=ona_s[r0_:r0_ + 128, :]), reads=["ona_s"], writes=["ona_t"])

            def a1(s):
                sq_rstd(gg_t[:], "gg_t", junk4[:], "junk4", 0, DNA)
                S.op("dve", lambda e: e.scalar_tensor_tensor(out=gn_t[:], in0=gg_t[:], scalar=st4[:, 1:2], in1=cn_sgu[:], op0=ALU.mult, op1=ALU.mult),
                     reads=["gg_t", "st4_1", "cn_sgu"], writes=["gn_t"])
                if s + 1 < 8:
                    ld("gg", s + 1)
                sq_rstd(ona_t[:], "ona_t", junk4[:], "junk4", 4, DNA)
                for g in range(4):
                    b = g % 2
                    pb = bank(b)
                    S.op("pe", (lambda pb, g: lambda e: e.matmul(pb, lhsT=wsT[:, g, :], rhs=gn_t[:, g * 512:(g + 1) * 512], start=True, stop=True))(pb, g),
                         reads=["wsT", "gn_t"], writes=[f"bank{b}"])
                    S.op("dve", (lambda pb, g: lambda e: e.scalar_tensor_tensor(out=gu_t[:, g * 512:(g + 1) * 512], in0=pb, scalar=sbT[:, g:g + 1], in1=gu_t[:, g * 512:(g + 1) * 512], op0=ALU.add, op1=ALU.mult))(pb, g),
                         reads=[f"bank{b}", "sbT", "gu_t"], writes=["gu_t"])
                sq_rstd(gu_t[:], "gu_t", junk4[:], "junk4", 2, DNA)

            def a2(s):
                S.op("dve", lambda e: e.scalar_tensor_tensor(out=oc_tok[:, 0:DNA], in0=ona_t[:], scalar=st4[:, 5:6], in1=cn_na[:], op0=ALU.mult, op1=ALU.mult),
                     reads=["ona_t", "st4_5", "cn_na"], writes=["oc_lo"])
                if s + 1 < 8:
                    ld("ona", s + 1)
                S.op("dve", lambda e: e.scalar_tensor_tensor(out=oc_tok[:, DNA:D], in0=gu_t[:], scalar=st4[:, 3:4], in1=cn_gs[:], op0=ALU.mult, op1=ALU.mult),
                     reads=["gu_t", "st4_3", "cn_gs"], writes=["oc_hi"])
                if s + 1 < 8:
                    ld("gu", s + 1)

            def b_(s):
                transpose_tile(oc_tok, (lambda g: "oc_lo" if g < 2 else "oc_hi"), ocT, "ocT", s * 128, 32, (6, 7))

            ld("gg", 0)
            ld("ona", 0)
            ld("gu", 0)
            for s in range(9):
                if s < 8:
                    a1(s)
                if s >= 1:
                    b_(s - 1)
                if s < 8:
                    a2(s)

        accb = 0
        nx = 0
        for half in range(2):
            t00 = half * 1024
            for p_ in range(2):
                load_w(wC[p_], f"wC{p_}", f"wC{p_}", wout_v, p_ * 512)
            p4_prologue(t00)
            S.fence()
            for ob_ in range(8):
                sl = ob_ % 2
                gb = g1b[ob_ % 2]
                gbr = f"g1b{ob_ % 2}"
                S.dma("sp", gbr, (lambda gb, ob_: lambda e: e.dma_start(out=gb[:], in_=modbc[2, :, ob_ * 512:(ob_ + 1) * 512]))(gb, ob_), reads=["modbc"], writes=[gbr])
                W = wC[sl]
                for s in range(8):
                    b = accb % 4
                    accb += 1
                    pb = bank(b)
                    for kc in range(32):
                        S.op("pe", (lambda pb, W, kc, s: lambda e: e.matmul(pb, lhsT=ocT[:, kc, s * 128:(s + 1) * 128], rhs=W[:, kc, :], start=(kc == 0), stop=(kc == 31)))(pb, W, kc, s),
                             reads=[f"wC{sl}", "ocT"], writes=[f"bank{b}"])
                    xb = xb4[nx % 3]
                    xbr = f"xb4_{nx % 3}"
                    tt = tt4[nx % 2]
                    ttr = f"tt4_{nx % 2}"
                    nx += 1
                    r0_ = t00 + s * 128
                    tsub = half * 8 + s
                    S.dma("sp", xbr, (lambda xb, r0_, ob_: lambda e: e.dma_start(out=xb[:], in_=xo[r0_:r0_ + 128, ob_ * 512:(ob_ + 1) * 512]))(xb, r0_, ob_), writes=[xbr])
                    S.op("dve", (lambda tt, pb, gb: lambda e: e.tensor_tensor(out=tt[:], in0=pb, in1=gb[:], op=ALU.mult))(tt, pb, gb), reads=[f"bank{b}", gbr], writes=[ttr])
                    S.op("pool", (lambda xb, tt: lambda e: e.tensor_tensor(out=xb[:], in0=tt[:], in1=xb[:], op=ALU.add))(xb, tt), reads=[ttr, xbr], writes=[xbr])
                    S.op("act", (lambda xb, tsub, ob_: lambda e: e.activation(out=junk4b[:], in_=xb[:], func=AF.Square, accum_out=stat1[:, tsub, ob_:ob_ + 1]))(xb, tsub, ob_),
                         reads=[xbr], writes=["junk4b", "stat1"])
                    S.dma("sp", xbr, (lambda xb, r0_, ob_: lambda e: e.dma_start(out=x1_s[r0_:r0_ + 128, ob_ * 512:(ob_ + 1) * 512], in_=xb[:]))(xb, r0_, ob_), reads=[xbr], writes=["x1_s"], accum=True)
                if ob_ + 2 < 8:
                    load_w(wC[sl], f"wC{sl}", f"wC{sl}", wout_v, (ob_ + 2) * 512)
            S.fence()
        S.op("dve", lambda e: e.reduce_sum(out=rstd2[:], in_=stat1[:], axis=AX.X), reads=["stat1"], writes=["rstd2"])
        rstd_from_ss(rstd2[:], rstd2[:], D, ["rstd2"], ["rstd2"])
        S.fence()
        if PH_STOP <= 3:
            return finish(nc, S, st, out)

        A = Arena(nc, PBASE)
        h2T = A.t("h2T", [128, 32, 1024], BF16)
        OV5 = A.off
        gwb5 = A.t("gwb5", [128, D], F32)
        shb5 = A.t("shb5", [128, D], F32)
        AO = Arena(nc, OV5)
        wE5 = [AO.t(f"wE5_{i}", [128, 8, 512], BF16) for i in range(2)]
        lhsE5 = AO.t("lhsE5", [128, 32, 128], BF16)
        bE5 = [AO.t(f"bE5_{i}", [128, 512], F32) for i in range(2)]
        eE5 = [AO.t(f"eE5_{i}", [128, 512], F32) for i in range(2)]
        assert AO.off <= A.off
        cTs5 = A.t("cTs5", [128, 32, 2], F32)

        def ada5_load(q):
            j = 40 + q // 4
            qt = q % 4
            sl = q % 2
            S.dma("pool", f"wE5_{sl}", lambda e: e.dma_start(out=wE5[sl][:], in_=wada_v[:, qt * 8:(qt + 1) * 8, j * 512:(j + 1) * 512]), writes=[f"wE5_{sl}"])

        def ada5_quarter(q, qend):
            n = q // 4
            qt = q % 4
            j = 40 + n
            cb = j % 8
            bsl = n % 2
            sl = q % 2
            pb = bank(4)
            if qt == 0:
                bcast_load(bE5[bsl][:], f"bE5_{bsl}", f"bE5_{bsl}", b_ada[j * 512:(j + 1) * 512])
            for k in range(8):
                kc = qt * 8 + k
                S.op("pe", (lambda k, kc: lambda e: e.matmul(pb, lhsT=lhsE5[:, kc, :], rhs=wE5[sl][:, k, :], start=(kc == 0), stop=(kc == 31)))(k, kc),
                     reads=[f"wE5_{sl}", "lhsE5"], writes=["bank4"])
            if q + 2 < qend:
                ada5_load(q + 2)
            if qt == 3:
                e_t = eE5[bsl]
                S.op("dve", lambda e: e.tensor_tensor(out=e_t[:], in0=pb, in1=bE5[bsl][:], op=ALU.add), reads=["bank4", f"bE5_{bsl}"], writes=[f"eE5_{bsl}"])
                S.dma("sp", f"eE5_{bsl}", lambda e: e.dma_start(out=modbc[5, :, cb * 512:(cb + 1) * 512], in_=e_t[:]), reads=[f"eE5_{bsl}"], writes=["modbc"], accum=True)

        wD = [A.t(f"wD{i}", [128, 32, 512], BF16) for i in range(2)]
        UB5 = A.off
        xs5 = [A.t(f"xs5_{i}", [128, D], F32) for i in range(2)]
        hb5 = A.t("hb5", [128, D], BF16)
        A2 = Arena(nc, UB5)
        rl5 = [A2.t(f"rl5_{i}", [128, 512], F32) for i in range(3)]
        hst5 = [A2.t(f"hst5_{i}", [128, 4, 1024], BF16) for i in range(2)]
        wff1_v = _rows(w_ff1)
        accb = 0
        nr5 = 0
        for half in range(2):
            t00 = half * 1024
            for p_ in range(2):
                load_w(wD[p_], f"wD{p_}", f"wD{p_}", wff1_v, p_ * 512)
            prologue(x1_s[t00:t00 + 1024, :], 8, lambda s: (4, 3), h2T, gwb5, shb5, xs5, hb5, None,
                     rstd_ext=(lambda half: lambda s: rstd2[:, half * 8 + s:half * 8 + s + 1])(half), force_reload=True)
            S.fence()
            S.dma("sp", "cTs5", lambda e: e.dma_start(out=cTs5[:], in_=cT), writes=["cTs5"])
            S.op("act", lambda e: e.activation(out=cTs5[:], in_=cTs5[:], func=AF.Silu), reads=["cTs5"], writes=["cTs5"])
            S.op("dve", lambda e: e.tensor_copy(out=lhsE5[:], in_=cTs5[:, :, 0:1].to_broadcast([128, 32, 128])), reads=["cTs5"], writes=["lhsE5"])
            q0 = half * 16
            ada5_load(q0)
            ada5_load(q0 + 1)
            for pi in range(32):
                sl = pi % 2
                if pi % 2 == 1:
                    ada5_quarter(q0 + pi // 2, q0 + 16)
                W = wD[sl]
                hs = hst5[pi % 2]
                hsr = f"hst5_{pi % 2}"
                for hb_ in range(4):
                    for tb in range(2):
                        b = accb % 4
                        accb += 1
                        pb = bank(b)
                        for kc in range(32):
                            S.op("pe", (lambda pb, W, kc, hb_, tb: lambda e: e.matmul(pb, lhsT=W[:, kc, hb_ * 128:(hb_ + 1) * 128], rhs=h2T[:, kc, tb * 512:(tb + 1) * 512], start=(kc == 0), stop=(kc == 31)))(pb, W, kc, hb_, tb),
                                 reads=[f"wD{sl}", "hT"], writes=[f"bank{b}"])
                        rl = rl5[nr5 % 3]
                        rlr = f"rl5_{nr5 % 3}"
                        nr5 += 1
                        S.op("act", (lambda rl, pb: lambda e: e.activation(out=rl[:], in_=pb, func=AF.Relu))(rl, pb), reads=[f"bank{b}"], writes=[rlr])
                        eng = "pool" if (nr5 % 2) else "dve"
                        S.op(eng, (lambda rl, hs, hb_, tb: lambda e: e.tensor_tensor(out=hs[:, hb_, tb * 512:(tb + 1) * 512], in0=rl[:], in1=rl[:], op=ALU.mult))(rl, hs, hb_, tb),
                             reads=[rlr], writes=[hsr])
                S.dma("sp", hsr, (lambda hs, pi, t00: lambda e: e.dma_start(out=hid_s[pi * 512:(pi + 1) * 512, t00:t00 + 1024].rearrange("(hb p) t -> p hb t", p=128), in_=hs[:]))(hs, pi, t00),
                      reads=[hsr], writes=["hid_s"], accum=True)
                if pi + 2 < 32:
                    load_w(wD[sl], f"wD{sl}", f"wD{sl}", wff1_v, (pi + 2) * 512)
            S.fence()
        if PH_STOP <= 4:
            return finish(nc, S, st, out)

        A = Arena(nc, PBASE)
        yacc = A.t("yacc", [128, 16, 1024], F32)
        hc6 = [A.t(f"hc6_{i}", [128, 8, 2048], BF16) for i in range(2)]
        w26 = [A.t(f"w26_{i}", [128, 8, 1024], BF16) for i in range(2)]
        x1b6 = [A.t(f"x1b6_{i}", [128, 1024], F32) for i in range(4)]
        tt6 = [A.t(f"tt6_{i}", [128, 1024], F32) for i in range(2)]
        g2b6 = [A.t(f"g2b6_{i}", [128, 1024], F32) for i in range(2)]
        junk6 = A.t("junk6", [128, 1024], BF16)
        wff2_v = _rows(w_ff2)
        accb = 0

        def p5b_load(cp, fc, sl):
            n0 = cp * 1024
            S.dma("sp", f"hc6_{sl}", lambda e: e.dma_start(out=hc6[sl][:], in_=hid_s[fc * 1024:(fc + 1) * 1024, :].rearrange("(kc p) t -> p kc t", p=128)),
                  reads=["hid_s"], writes=[f"hc6_{sl}"])
            load_w(w26[sl], f"w26_{sl}", f"w26_{sl}", wff2_v, n0, nkc=8, k0=fc * 8, ncol=1024)

        chunks6 = [(cp, fc) for cp in range(4) for fc in range(16)]
        for ci_ in range(2):
            p5b_load(chunks6[ci_][0], chunks6[ci_][1], ci_ % 2)

        def ep_load(e_):
            cp_, s_ = e_ // 16, e_ % 16
            xb = x1b6[e_ % 4]
            xbr = f"x1b6_{e_ % 4}"
            S.dma("sp", xbr, lambda e: e.dma_start(out=xb[:], in_=x1_s[s_ * 128:(s_ + 1) * 128, cp_ * 1024:(cp_ + 1) * 1024]), reads=["x1_s"], writes=[xbr])

        def epilogue6(e_):
            cp_, s_ = e_ // 16, e_ % 16
            xb = x1b6[e_ % 4]
            xbr = f"x1b6_{e_ % 4}"
            tt = tt6[e_ % 2]
            ttr = f"tt6_{e_ % 2}"
            g2b = g2b6[cp_ % 2]
            g2r = f"g2b6_{cp_ % 2}"
            ya = yacc[:, s_, :]
            yrs = [f"y{s_}_0", f"y{s_}_1"]
            S.op("dve", lambda e: e.tensor_tensor(out=tt[:], in0=ya, in1=g2b[:], op=ALU.mult), reads=yrs + [g2r], writes=[ttr])
            S.op("pool", lambda e: e.tensor_tensor(out=xb[:], in0=tt[:], in1=xb[:], op=ALU.add), reads=[ttr, xbr], writes=[xbr])

        def epilogue6b(e_):
            cp_, s_ = e_ // 16, e_ % 16
            xb = x1b6[e_ % 4]
            xbr = f"x1b6_{e_ % 4}"
            S.op("act", lambda e: e.activation(out=junk6[:], in_=xb[:], func=AF.Square, accum_out=stat2[:, s_, cp_:cp_ + 1]), reads=[xbr], writes=["junk6", "stat2"])
            S.dma("sp", xbr, lambda e: e.dma_start(out=x2_s[s_ * 128:(s_ + 1) * 128, cp_ * 1024:(cp_ + 1) * 1024], in_=xb[:]), reads=[xbr], writes=["x2_s"], accum=True)
            if e_ + 4 < 64 and (e_ + 4) // 16 == cp_:
                ep_load(e_ + 4)

        for ci_, (cp, fc) in enumerate(chunks6):
            n0 = cp * 1024
            g2b = g2b6[cp % 2]
            g2r = f"g2b6_{cp % 2}"
            if fc == 0:
                S.dma("sp", g2r, (lambda g2b, n0: lambda e: e.dma_start(out=g2b[:], in_=modbc[5, :, n0:n0 + 1024]))(g2b, n0), reads=["modbc"], writes=[g2r])
            if fc == 15:
                for e_ in range(cp * 16, cp * 16 + 4):
                    ep_load(e_)
            sl = ci_ % 2
            hc = hc6[sl]
            w2 = w26[sl]
            for s in range(16):
                if fc == 0 and cp > 0:
                    epilogue6((cp - 1) * 16 + s)
                    if s >= 1:
                        epilogue6b((cp - 1) * 16 + s - 1)
                for cb in range(2):
                    b = accb % 4
                    accb += 1
                    pb = bank(b)
                    for kc in range(8):
                        S.op("pe", (lambda pb, hc, w2, kc, s, cb: lambda e: e.matmul(pb, lhsT=hc[:, kc, s * 128:(s + 1) * 128], rhs=w2[:, kc, cb * 512:(cb + 1) * 512], start=(kc == 0), stop=(kc == 7)))(pb, hc, w2, kc, s, cb),
                             reads=[f"hc6_{sl}", f"w26_{sl}"], writes=[f"bank{b}"])
                    ya = yacc[:, s, cb * 512:(cb + 1) * 512]
                    yr = f"y{s}_{cb}"
                    if fc == 0:
                        S.op("act", (lambda ya, pb: lambda e: e.copy(out=ya, in_=pb))(ya, pb), reads=[f"bank{b}"], writes=[yr])
                    else:
                        S.op("dve", (lambda ya, pb: lambda e: e.tensor_tensor(out=ya, in0=pb, in1=ya, op=ALU.add))(ya, pb), reads=[f"bank{b}", yr], writes=[yr])
            if fc == 0 and cp > 0:
                epilogue6b((cp - 1) * 16 + 15)
            if ci_ + 2 < len(chunks6):
                p5b_load(chunks6[ci_ + 2][0], chunks6[ci_ + 2][1], sl)
        for s in range(16):
            epilogue6(48 + s)
            if s >= 1:
                epilogue6b(48 + s - 1)
        epilogue6b(63)
        S.fence()
        S.op("dve", lambda e: e.reduce_sum(out=rstd3[:], in_=stat2[:], axis=AX.X), reads=["stat2"], writes=["rstd3"])
        rstd_from_ss(rstd3[:], rstd3[:], D, ["rstd3"], ["rstd3"])
        S.fence()
        if PH_STOP <= 5:
            return finish(nc, S, st, out)

        A = Arena(nc, PBASE)
        fnb = A.t("fnb", [128, D], F32)
        x7 = [A.t(f"x7_{i}", [128, D], F32) for i in range(4)]
        bcast_load(fnb[:], "fnb", "fnb", fnw)
        for s in range(16):
            xt = x7[s % 4]
            xr = f"x7_{s % 4}"
            S.dma(("sp" if s % 2 == 0 else "pool"), xr, (lambda xt, s: lambda e: e.dma_start(out=xt[:], in_=x2_s[s * 128:(s + 1) * 128, :]))(xt, s), reads=["x2_s"], writes=[xr])
            S.op("dve", (lambda xt, s: lambda e: e.scalar_tensor_tensor(out=xt[:], in0=xt[:], scalar=rstd3[:, s:s + 1], in1=fnb[:], op0=ALU.mult, op1=ALU.mult))(xt, s),
                 reads=[xr, "rstd3", "fnb"], writes=[xr])
            S.dma("act", xr, (lambda xt, s: lambda e: e.dma_start(out=out[s * 128:(s + 1) * 128, :], in_=xt[:]))(xt, s), reads=[xr], writes=["out"], accum=True)
        return finish(nc, S, st, out)


def finish(nc, S, st, out):
    S.fence()
    stats = S.emit(st)
    print("sched stats", stats, flush=True)
    return nc


TABS = [(0, 0, 12), (1, 0, 12), (2, 4, 9), (14, 28, 9), (15, 28, 12)]


def _bias_index(half):
    R0 = 32 * half
    p = np.arange(128)
    hq = p // 64
    c = p % 64
    m = np.arange(768)
    irel = m // 64
    j = m % 64
    ro = np.zeros((5, 128, 768), np.int64)
    co = np.zeros((5, 128, 768), np.int64)
    valid = np.zeros((5, 128, 768), bool)
    for t, (i, B, nrows) in enumerate(TABS):
        sq = 4 + 2 * i + hq
        gq = R0 - 4 + sq
        r0 = np.clip(gq - 4, 0, 56)
        cs = np.clip(c - 8, 0, 48)
        gk = R0 - 4 + B + irel
        v = (irel[None, :] < nrows) & (gk[None, :] >= r0[:, None]) & (gk[None, :] < r0[:, None] + 8) \
            & (j[None, :] >= cs[:, None]) & (j[None, :] < cs[:, None] + 16)
        valid[t] = v
        ro[t] = np.clip(gk[None, :] - gq[:, None] + 7, 0, 14)
        co[t] = np.clip(j[None, :] - c[:, None] + 15, 0, 30)
    return ro, co, valid


_PROG = {}


def make_in_maps(inp):
    x = np.asarray(inp["x"], np.float32)
    c = np.asarray(inp["c"], np.float32)
    ctx = np.asarray(inp["ctx"], np.float32)
    c_ctx = np.asarray(inp["c_ctx"], np.float32)
    rpb = np.asarray(inp["rpb"], np.float32)[0]
    g = lambda k: np.ascontiguousarray(np.asarray(inp[k], np.float32)[0])
    shared = {
        "w_ada": g("w_ada"), "b_ada": g("b_ada"), "norm1_w": g("norm1_w"), "w_in": g("w_in"),
        "sgu_nw": g("sgu_norm_w"),
        "sgu_wT": np.ascontiguousarray(np.transpose(g("sgu_w"), (2, 0, 1))),
        "sgu_bT": np.ascontiguousarray(g("sgu_b").T),
        "gn_na": g("grp_norm_na"), "gn_sgu": g("grp_norm_sgu"), "w_out": g("w_out"),
        "norm2_w": g("norm2_w"), "w_ff1": g("w_ff1"), "w_ff2": g("w_ff2"),
        "fnw": np.ascontiguousarray(np.asarray(inp["final_norm_w"], np.float32)),
        "identf": np.eye(128, dtype=np.float32),
    }
    tabs = []
    for half in range(2):
        ro, co, valid = _bias_index(half)
        rx = np.ascontiguousarray(np.transpose(rpb[:, ro, co], (0, 2, 1, 3)))
        mk = np.ascontiguousarray(np.transpose(np.where(valid, np.float32(0.0), np.float32(NEG)), (1, 0, 2)))
        tabs.append((rx, mk.astype(np.float32)))
    maps = []
    for core in range(8):
        b, half = core // 2, core % 2
        R0 = 32 * half
        xg = x[b].reshape(64, 64, D)
        pre = np.clip(np.arange(R0 - 4, R0), 0, 63)
        post = np.clip(np.arange(R0 + 32, R0 + 36), 0, 63)
        xo = np.ascontiguousarray(xg[R0:R0 + 32].reshape(2048, D))
        xh = np.ascontiguousarray(np.concatenate([xg[pre].reshape(256, D), xg[post].reshape(256, D), ctx[b]], 0))
        cT = np.ascontiguousarray(np.stack([c[b], c_ctx], -1).reshape(32, 128, 2).transpose(1, 0, 2))
        m = dict(shared)
        m.update({"xo": xo, "xh": xh, "cT": cT, "rpbx": tabs[half][0], "maskx": tabs[half][1]})
        maps.append(m)
    return maps


def kernel(**inp):
    if "nc" not in _PROG:
        _PROG["nc"] = build_program()
    nc = _PROG["nc"]
    maps = make_in_maps(inp)
    res = run_bass_kernel_spmd(nc, maps, core_ids=list(range(8)))
    _PROG["res"] = res
    outp = np.empty((4, 4096, D), np.float32)
    for core in range(8):
        b, half = core // 2, core % 2
        outp[b, half * 2048:(half + 1) * 2048, :] = np.asarray(res.results[core]["out"], np.float32)
    return outp
```

```python
import os
import numpy as np
from contextlib import ExitStack
import concourse.bass as bass
import concourse.mybir as mybir
from concourse.bass_utils import run_bass_kernel_spmd

F32 = mybir.dt.float32
BF16 = mybir.dt.bfloat16
AF = mybir.ActivationFunctionType
ALU = mybir.AluOpType
AX = mybir.AxisListType

D = 4096
DNA = 2048
DIN = 10240
DFF = 16384
NH = 16
EPS = 1e-6
NEG = -30000.0
SCALE = 128.0 ** -0.5
DEBUG = bool(int(os.environ.get("MK_DEBUG", "0")))
PH_STOP = int(os.environ.get("MK_STOP", "99"))


class _Op:
    __slots__ = ("eng", "fn", "deps", "signal", "val", "semkey", "is_dma")

    def __init__(self, eng, fn, deps, semkey=None):
        self.eng = eng
        self.fn = fn
        self.deps = deps
        self.signal = False
        self.val = 0
        self.semkey = semkey
        self.is_dma = semkey is not None


class Sched:
    ENGS = ("pe", "act", "dve", "pool", "sp")

    def __init__(self, nc):
        self.nc = nc
        self.q = {e: [] for e in self.ENGS}
        self.res = {}
        self.dma_count = {}
        self.last = {e: None for e in self.ENGS}
        self.lastdma = {}

    def _R(self, name):
        r = self.res.get(name)
        if r is None:
            r = ({}, {})
            self.res[name] = r
        return r

    def _add(self, eng, stream, fn, reads, writes, semkey=None, accum=False):
        deps = []
        for r in reads:
            deps.extend(self._R(r)[0].values())
        for w in writes:
            R = self._R(w)
            if not accum:
                deps.extend(R[0].values())
                deps.extend(R[1].values())
        seen = set()
        dd = []
        for d in deps:
            if id(d) in seen:
                continue
            seen.add(id(d))
            if eng == "pe" and semkey is None and (not d.is_dma) and d.eng == "pe":
                continue
            dd.append(d)
            if not d.is_dma:
                d.signal = True
        op = _Op(eng, fn, dd, semkey)
        if semkey is not None:
            c = self.dma_count.get(semkey, 0) + 16
            self.dma_count[semkey] = c
            op.val = c
            self.lastdma[semkey] = op
        self.q[eng].append(op)
        if semkey is None:
            self.last[eng] = op
        for r in reads:
            self._R(r)[1][stream] = op
        for w in writes:
            R = self._R(w)
            if not accum:
                R[0].clear()
                R[1].clear()
            R[0][stream] = op
        return op

    def op(self, eng, fn, reads=(), writes=()):
        return self._add(eng, eng, fn, reads, writes)

    def dma(self, eng, key, fn, reads=(), writes=(), accum=False):
        return self._add(eng, "dma:" + key, fn, reads, writes, semkey=key, accum=accum)

    def fence(self):
        allops = [o for o in self.last.values() if o is not None]
        allops.extend(self.lastdma.values())
        for o in allops:
            if not o.is_dma:
                o.signal = True
        for e in self.ENGS:
            deps = [o for o in allops if not (e == "pe" and (not o.is_dma) and o.eng == "pe")]
            self.q[e].append(_Op(e, None, deps))
        self.res = {}

    def emit(self, stack):
        nc = self.nc
        esem = {e: stack.enter_context(nc.semaphore("e_" + e)) for e in self.ENGS}
        dsem = {k: stack.enter_context(nc.semaphore("d_" + k)) for k in self.dma_count}
        for e in self.ENGS:
            c = 0
            for o in self.q[e]:
                if (not o.is_dma) and o.signal:
                    c += 1
                    o.val = c
            assert c < 60000, (e, c)
        block = stack.enter_context(nc.Block())
        stats = {}

        def make(ename):
            def body(eng):
                waited = {}
                nw = 0
                for o in self.q[ename]:
                    for d in o.deps:
                        if d.is_dma:
                            key = "d_" + d.semkey
                            sem = dsem[d.semkey]
                        else:
                            key = "e_" + d.eng
                            sem = esem[d.eng]
                        if waited.get(key, 0) >= d.val:
                            continue
                        eng.wait_ge(sem, d.val)
                        waited[key] = d.val
                        nw += 1
                    if o.fn is None:
                        continue
                    ins = o.fn(eng)
                    if o.is_dma:
                        ins.then_inc(dsem[o.semkey], 16)
                    elif o.signal:
                        ins.then_inc(esem[ename], 1)
                stats[ename] = (len(self.q[ename]), nw)
            return body

        block.tensor(make("pe"))
        block.scalar(make("act"))
        block.vector(make("dve"))
        block.gpsimd(make("pool"))
        block.sync(make("sp"))
        return stats


class Arena:
    LO = 16512
    HI = 229344
    _n = [0]

    def __init__(self, nc, start=None):
        self.nc = nc
        self.off = self.LO if start is None else start

    def t(self, name, shape, dt):
        esz = 2 if dt == BF16 else 4
        n = 1
        for s in shape[1:]:
            n *= s
        nbytes = n * esz
        off = (self.off + 63) // 64 * 64
        assert off + nbytes <= self.HI, (name, off, nbytes)
        self._n[0] += 1
        h = self.nc.alloc_sbuf_tensor_at(f"{name}_{self._n[0]}", list(shape), dt, offset=off)
        self.off = off + nbytes
        return h


def _rows(ap2d):
    return ap2d.rearrange("(kc p) c -> p kc c", p=128)


def build_program():
    nc = bass.Bass("TRN2", target_bir_lowering=False)
    din = lambda name, shape, dt=F32: nc.dram_tensor(name, list(shape), dt, kind="ExternalInput").ap()
    skind = "ExternalOutput" if DEBUG else "Internal"
    dscr = lambda name, shape, dt: nc.dram_tensor(name, list(shape), dt, kind=skind).ap()

    xo = din("xo", [2048, D])
    xh = din("xh", [768, D])
    cT = din("cT", [128, 32, 2])
    w_ada = din("w_ada", [D, 6 * D])
    b_ada = din("b_ada", [6 * D])
    norm1_w = din("norm1_w", [D])
    w_in = din("w_in", [D, DIN])
    rpbx = din("rpbx", [NH, 128, 5, 768])
    maskx = din("maskx", [128, 5, 768])
    sgu_nw = din("sgu_nw", [DNA])
    sgu_wT = din("sgu_wT", [128, 4, 128])
    sgu_bT = din("sgu_bT", [128, 4])
    gn_na = din("gn_na", [DNA])
    gn_sgu = din("gn_sgu", [DNA])
    w_out = din("w_out", [D, D])
    norm2_w = din("norm2_w", [D])
    w_ff1 = din("w_ff1", [D, DFF])
    w_ff2 = din("w_ff2", [DFF, D])
    fnw = din("fnw", [D])
    identf = din("identf", [128, 128])
    out = nc.dram_tensor("out", [2048, D], F32, kind="ExternalOutput").ap()

    modbc = dscr("modbc", [8, 128, D], F32)
    qT_s = dscr("qT_s", [NH, 128, 2048], BF16)
    kT_s = dscr("kT_s", [NH, 128, 2816], BF16)
    v_s = dscr("v_s", [2816, DNA], BF16)
    gu_s = dscr("gu_s", [2048, DNA], F32)
    gg_s = dscr("gg_s", [2048, DNA], F32)
    ona_s = dscr("ona_s", [2048, DNA], F32)
    x1_s = dscr("x1_s", [2048, D], F32)
    hid_s = dscr("hid_s", [DFF, 2048], BF16)
    x2_s = dscr("x2_s", [2048, D], F32)

    with ExitStack() as st:
        S = Sched(nc)
        PS = [nc.alloc_psum_tensor(f"ps{i}", [128, 1024], F32) for i in range(4)]

        def bank(i):
            return PS[i // 2][:, (i % 2) * 512:(i % 2) * 512 + 512]

        def bank_bf(i):
            return PS[i // 2][:, (i % 2) * 512:(i % 2) * 512 + 512].bitcast(BF16)

        A0 = Arena(nc)
        ident = A0.t("ident", [128, 128], BF16)
        identf_sb = A0.t("identf", [128, 128], F32)
        stat1 = A0.t("stat1", [128, 16, 8], F32)
        stat2 = A0.t("stat2", [128, 16, 4], F32)
        rstd2 = A0.t("rstd2", [128, 16], F32)
        rstd3 = A0.t("rstd3", [128, 16], F32)
        PBASE = A0.off

        S.dma("sp", "identf", lambda e: e.dma_start(out=identf_sb[:], in_=identf), writes=["identf"])
        S.op("dve", lambda e: e.tensor_copy(out=ident[:], in_=identf_sb[:]), reads=["identf"], writes=["ident"])

        def rstd_from_ss(ss_ap, out_ap, n, rs, ws):
            S.op("act", lambda e: e.activation(out=out_ap, in_=ss_ap, func=AF.Sqrt, bias=EPS, scale=1.0 / n),
                 reads=rs, writes=ws)
            S.op("dve", lambda e: e.reciprocal(out=out_ap, in_=out_ap), reads=ws, writes=ws)

        def transpose_tile(src, srcres, dstT, dstres, tok0, nkc, tpb):
            for g in range(nkc // 8):
                b = tpb[g % 2]
                pb = bank_bf(b)
                for j in range(8):
                    kc = g * 8 + j
                    S.op("pe", (lambda kc, j, pb, src: lambda e: e.transpose(out=pb[:, j * 128:(j + 1) * 128], in_=src[:, kc * 128:(kc + 1) * 128], identity=ident[:]))(kc, j, pb, src),
                         reads=[(srcres(g) if callable(srcres) else srcres), "ident"], writes=[f"bank{b}"])
                eng = "act" if g % 2 == 0 else "dve"
                dst = dstT[:, g * 8:(g + 1) * 8, tok0:tok0 + 128]
                srcv = pb.rearrange("p (j t) -> p j t", j=8)
                if eng == "act":
                    S.op("act", (lambda dst, srcv: lambda e: e.copy(out=dst, in_=srcv))(dst, srcv), reads=[f"bank{b}"], writes=[dstres])
                else:
                    S.op("dve", (lambda dst, srcv: lambda e: e.tensor_copy(out=dst, in_=srcv))(dst, srcv), reads=[f"bank{b}"], writes=[dstres])

        def load_w(slot, slotres, key, wview, c0, nkc=32, k0=0, ncol=512):
            S.dma("pool", key, (lambda slot, wview, c0: lambda e: e.dma_start(out=slot[:, 0:nkc, 0:ncol], in_=wview[:, k0:k0 + nkc, c0:c0 + ncol]))(slot, wview, c0),
                  writes=[slotres])

        def bcast_load(dst, dstres, key, vec_ap):
            S.dma("sp", key, lambda e: e.dma_start(out=dst, in_=vec_ap.partition_broadcast(128)), writes=[dstres])

        def make_ada(AO, pfx, jlist, pbank):
            wE_ = [AO.t(f"{pfx}w{i}", [128, 8, 512], BF16) for i in range(2)]
            lhs_ = AO.t(f"{pfx}lhs", [128, 32, 128], BF16)
            bE_ = AO.t(f"{pfx}b", [128, 512], F32)
            nE_ = AO.t(f"{pfx}n", [128, 512], F32)
            eE_ = AO.t(f"{pfx}e", [128, 512], F32)
            cT_ = AO.t(f"{pfx}c", [128, 32, 2], F32)
            nq = 4 * len(jlist)
            wv = _rows(w_ada)

            def load(q, qend):
                j = jlist[q // 4]
                qt = q % 4
                sl = q % 2
                S.dma("pool", f"{pfx}w{sl}", lambda e: e.dma_start(out=wE_[sl][:], in_=wv[:, qt * 8:(qt + 1) * 8, j * 512:(j + 1) * 512]), writes=[f"{pfx}w{sl}"])

            def prep(q0, qend):
                S.dma("sp", f"{pfx}c", lambda e: e.dma_start(out=cT_[:], in_=cT), writes=[f"{pfx}c"])
                S.op("act", lambda e: e.activation(out=cT_[:], in_=cT_[:], func=AF.Silu), reads=[f"{pfx}c"], writes=[f"{pfx}c"])
                S.op("dve", lambda e: e.tensor_copy(out=lhs_[:], in_=cT_[:, :, 0:1].to_broadcast([128, 32, 128])), reads=[f"{pfx}c"], writes=[f"{pfx}lhs"])
                for q in range(q0, min(q0 + 2, qend)):
                    load(q, qend)

            def quarter(q, qend):
                j = jlist[q // 4]
                qt = q % 4
                which = j // 8
                cb = j % 8
                sl = q % 2
                pb = bank(pbank)
                if qt == 0:
                    bcast_load(bE_[:], f"{pfx}b", f"{pfx}b", b_ada[j * 512:(j + 1) * 512])
                    if which == 4:
                        bcast_load(nE_[:], f"{pfx}n", f"{pfx}n", norm2_w[cb * 512:(cb + 1) * 512])
                for k in range(8):
                    kc = qt * 8 + k
                    S.op("pe", (lambda k, kc: lambda e: e.matmul(pb, lhsT=lhs_[:, kc, :], rhs=wE_[sl][:, k, :], start=(kc == 0), stop=(kc == 31)))(k, kc),
                         reads=[f"{pfx}w{sl}", f"{pfx}lhs"], writes=[f"bank{pbank}"])
                if q + 2 < qend:
                    load(q + 2, qend)
                if qt == 3:
                    S.op("dve", lambda e: e.tensor_tensor(out=eE_[:], in0=pb, in1=bE_[:], op=ALU.add), reads=[f"bank{pbank}", f"{pfx}b"], writes=[f"{pfx}e"])
                    if which == 4:
                        S.op("dve", lambda e: e.scalar_tensor_tensor(out=eE_[:], in0=eE_[:], scalar=1.0, in1=nE_[:], op0=ALU.add, op1=ALU.mult),
                             reads=[f"{pfx}e", f"{pfx}n"], writes=[f"{pfx}e"])
                    S.dma("sp", f"{pfx}e", lambda e: e.dma_start(out=modbc[which, :, cb * 512:(cb + 1) * 512], in_=eE_[:]), reads=[f"{pfx}e"], writes=["modbc"], accum=True)
            return prep, quarter, nq

        A = Arena(nc, PBASE)
        cTs = A.t("cTs", [128, 32, 2], F32)
        lhs_lat = A.t("lhs_lat", [128, 32, 128], BF16)
        lhs_ctx = A.t("lhs_ctx", [128, 32, 128], BF16)
        wA_ = [A.t(f"wA_{i}", [128, 32, 512], BF16) for i in range(3)]
        bA_ = [A.t(f"bA_{i}", [128, 512], F32) for i in range(2)]
        nA_ = [A.t(f"nA_{i}", [128, 512], F32) for i in range(2)]
        eA_ = [A.t(f"ev{i}", [128, 512], F32) for i in range(4)]
        S.dma("sp", "cTs", lambda e: e.dma_start(out=cTs[:], in_=cT), writes=["cTs"])
        S.op("act", lambda e: e.activation(out=cTs[:], in_=cTs[:], func=AF.Silu), reads=["cTs"], writes=["cTs"])
        S.op("dve", lambda e: e.tensor_copy(out=lhs_lat[:], in_=cTs[:, :, 0:1].to_broadcast([128, 32, 128])), reads=["cTs"], writes=["lhs_lat"])
        S.op("dve", lambda e: e.tensor_copy(out=lhs_ctx[:], in_=cTs[:, :, 1:2].to_broadcast([128, 32, 128])), reads=["cTs"], writes=["lhs_ctx"])
        wada_v = _rows(w_ada)
        KS = 19
        fst = [A.t(f"fst{i}", [128, 32 - KS, 512], F32) for i in range(2)]
        dest_lat = {0: 0, 1: 1, 2: 2, 3: 3, 4: 4, 5: 5}
        evi = 0
        NB0 = 16

        def fst_load(j):
            f_ = fst[j % 2]
            S.dma("act", f"fst{j % 2}", lambda e: e.dma_start(out=f_[:], in_=wada_v[:, KS:32, j * 512:(j + 1) * 512]), writes=[f"fst{j % 2}"])

        for j in range(NB0):
            which = j // 8
            cb = j % 8
            sl = j % 3
            ra, rb = f"wA_{sl}a", f"wA_{sl}b"
            S.dma("pool", f"wA_{sl}", (lambda sl, j: lambda e: e.dma_start(out=wA_[sl][:, 0:KS, :], in_=wada_v[:, 0:KS, j * 512:(j + 1) * 512]))(sl, j), writes=[ra])
            f_ = fst[j % 2]
            fr = f"fst{j % 2}"
            if j == 0:
                fst_load(0)
            if j + 1 < NB0:
                fst_load(j + 1)
            S.op("act", (lambda f_, sl: lambda e: e.copy(out=wA_[sl][:, KS:32, :], in_=f_[:]))(f_, sl), reads=[fr], writes=[rb])
            bsl = j % 2
            bcast_load(bA_[bsl][:], f"bA_{bsl}", f"bA_{bsl}", b_ada[j * 512:(j + 1) * 512])
            is_sc = which in (1, 4)
            if is_sc:
                nv = norm1_w if which == 1 else norm2_w
                bcast_load(nA_[bsl][:], f"nA_{bsl}", f"nA_{bsl}", nv[cb * 512:(cb + 1) * 512])
            variants = [("lat", lhs_lat, dest_lat[which])]
            if which < 2:
                variants.append(("ctx", lhs_ctx, 6 + which))
            for (vn, lhs, dst_i) in variants:
                b = evi % 4
                pb = bank(b)
                for kc in range(32):
                    S.op("pe", (lambda pb, lhs, sl, kc: lambda e: e.matmul(pb, lhsT=lhs[:, kc, :], rhs=wA_[sl][:, kc, :], start=(kc == 0), stop=(kc == 31)))(pb, lhs, sl, kc),
                         reads=[(ra if kc < KS else rb), "lhs_" + vn], writes=[f"bank{b}"])
                e_t = eA_[b]
                S.op("dve", (lambda e_t, pb, bsl: lambda e: e.tensor_tensor(out=e_t[:], in0=pb, in1=bA_[bsl][:], op=ALU.add))(e_t, pb, bsl),
                     reads=[f"bank{b}", f"bA_{bsl}"], writes=[f"ev{b}"])
                if is_sc:
                    S.op("dve", (lambda e_t, bsl: lambda e: e.scalar_tensor_tensor(out=e_t[:], in0=e_t[:], scalar=1.0, in1=nA_[bsl][:], op0=ALU.add, op1=ALU.mult))(e_t, bsl),
                         reads=[f"ev{b}", f"nA_{bsl}"], writes=[f"ev{b}"])
                S.dma("sp", f"ev{b}", (lambda e_t, dst_i, cb: lambda e: e.dma_start(out=modbc[dst_i, :, cb * 512:(cb + 1) * 512], in_=e_t[:]))(e_t, dst_i, cb),
                      reads=[f"ev{b}"], writes=["modbc"], accum=True)
                evi += 1
        S.fence()
        if PH_STOP <= 0:
            return finish(nc, S, st, out)

        A = Arena(nc, PBASE)
        hT = A.t("hT", [128, 32, 1024], BF16)
        OV1 = A.off
        gwb = A.t("gwb", [128, D], F32)
        shb = A.t("shb", [128, D], F32)
        AO1 = Arena(nc, OV1)
        ada1_prep, ada1_quarter, ada1_nq = make_ada(AO1, "a1", list(range(24, 40)), 4)
        assert AO1.off <= A.off
        wsl = [A.t(f"w1sl{i}", [128, 32, 512], BF16) for i in range(2)]
        UB = A.off
        xs = [A.t(f"xs{i}", [128, D], F32) for i in range(2)]
        hb = A.t("hb", [128, D], BF16)
        ss = A.t("ss", [128, 8], F32)
        junkp = A.t("junkp", [128, 1024], BF16)
        A2 = Arena(nc, UB)
        stq = [A2.t(f"stq{i}", [128, 4, 1024], BF16) for i in range(2)]
        stv = [A2.t(f"stv{i}", [128, 8, 512], BF16) for i in range(2)]
        stg = [A2.t(f"stg{i}", [128, 512], F32) for i in range(4)]
        win_v = _rows(w_in)

        mod_loaded = {}

        def prologue(src_ap, nsub, sub_mod, hT_t, gw_t, sh_t, xs, hb, ss, rstd_ext=None, junk=None, force_reload=False):
            state = {"cur": None if force_reload else mod_loaded.get(id(gw_t))}
            HL = 1024

            def a1(s):
                m = sub_mod(s)
                mod_loaded[id(gw_t)] = m
                if m != state["cur"]:
                    S.dma("sp", "gwb", lambda e: e.dma_start(out=gw_t[:], in_=modbc[m[0]]), reads=["modbc"], writes=["gwb"])
                    S.dma("sp", "shb", lambda e: e.dma_start(out=sh_t[:], in_=modbc[m[1]]), reads=["modbc"], writes=["shb"])
                    state["cur"] = m
                x_t = xs[s % 2]
                xr = f"xs{s % 2}"
                S.dma("sp", xr, lambda e: e.dma_start(out=x_t[:], in_=src_ap[s * 128:(s + 1) * 128, :]), writes=[xr])
                if rstd_ext is None:
                    for c4 in range(4):
                        S.op("act", (lambda c4: lambda e: e.activation(out=junk[:], in_=x_t[:, c4 * 1024:(c4 + 1) * 1024], func=AF.Square, accum_out=ss[:, c4:c4 + 1]))(c4),
                             reads=[xr], writes=["junkp", "ss4"])
                    S.op("dve", lambda e: e.reduce_sum(out=ss[:, 4:5], in_=ss[:, 0:4], axis=AX.X), reads=["ss4"], writes=["ss"])
                    rstd_from_ss(ss[:, 4:5], ss[:, 5:6], D, ["ss"], ["ss1"])
                    rs_ap = ss[:, 5:6]
                    rs_res = "ss1"
                else:
                    rs_ap = rstd_ext(s)
                    rs_res = "rstd_ext"
                S.op("dve", lambda e: e.scalar_tensor_tensor(out=x_t[:], in0=x_t[:], scalar=rs_ap, in1=gw_t[:], op0=ALU.mult, op1=ALU.mult),
                     reads=[xr, rs_res, "gwb"], writes=[xr])

            def a2(s):
                x_t = xs[s % 2]
                xr = f"xs{s % 2}"
                S.op("pool", lambda e: e.tensor_tensor(out=hb[:, 0:HL], in0=x_t[:, 0:HL], in1=sh_t[:, 0:HL], op=ALU.add), reads=[xr, "shb"], writes=["hb_lo"])
                S.op("dve", lambda e: e.tensor_tensor(out=hb[:, HL:D], in0=x_t[:, HL:D], in1=sh_t[:, HL:D], op=ALU.add), reads=[xr, "shb"], writes=["hb_hi"])

            def b_(s):
                transpose_tile(hb, (lambda g: "hb_lo" if g < 1 else "hb_hi"), hT_t, "hT", s * 128, 32, (6, 7))

            for s in range(nsub + 1):
                if s < nsub:
                    a1(s)
                if s >= 1:
                    b_(s - 1)
                if s < nsub:
                    a2(s)

        def evac(eng, dst, src, reads, writes, func=None):
            if eng == "act":
                if func is None:
                    S.op("act", lambda e: e.copy(out=dst, in_=src), reads=reads, writes=writes)
                else:
                    S.op("act", lambda e: e.activation(out=dst, in_=src, func=func), reads=reads, writes=writes)
            else:
                S.op("dve", lambda e: e.tensor_copy(out=dst, in_=src), reads=reads, writes=writes)

        def p1_pass(pname, src_ap, nsub, sub_mod, pieces, tokblocks, kcol0, qcol0, vrows, urow0, adaq=None):
            colof = lambda kind, idx: {"q": 0, "k": 2048, "v": 4096, "u": 6144, "g": 8192}[kind] + idx * 512
            for p_ in range(2):
                load_w(wsl[p_], f"w1sl{p_}", f"w1sl{p_}", win_v, colof(*pieces[p_]))
            prologue(src_ap, nsub, sub_mod, hT, gwb, shb, xs, hb, ss, junk=junkp, force_reload=True)
            S.fence()
            if adaq is not None:
                ada1_prep(adaq[0], adaq[1])
                adan = adaq[0]
            accb = 0
            nq = 0
            nv = 0
            ng = 0
            npieces = len(pieces)
            for pi, (kind, idx) in enumerate(pieces):
                sl = pi % 2
                col0 = {"q": 0, "k": 2048, "v": 4096, "u": 6144, "g": 8192}[kind] + idx * 512
                W = wsl[sl]
                wres = f"w1sl{sl}"
                if kind in ("q", "k"):
                    sq = stq[nq % 2]
                    sqr = f"stq{nq % 2}"
                    nq += 1
                    for hh in range(4):
                        for (t0, n) in tokblocks:
                            b = accb % 4
                            accb += 1
                            pb = bank(b)
                            for kc in range(32):
                                S.op("pe", (lambda pb, W, kc, hh, t0, n, hT=hT: lambda e: e.matmul(pb[:, 0:n], lhsT=W[:, kc, hh * 128:(hh + 1) * 128], rhs=hT[:, kc, t0:t0 + n], start=(kc == 0), stop=(kc == 31)))(pb, W, kc, hh, t0, n),
                                     reads=[wres, "hT"], writes=[f"bank{b}"])
                            evac("act" if (accb % 2) else "dve", sq[:, hh, t0:t0 + n], pb[:, 0:n], [f"bank{b}"], [sqr])
                    h0 = idx * 4
                    dst_s = qT_s if kind == "q" else kT_s
                    cols = qcol0 if kind == "q" else kcol0
                    for bi, (t0, n) in enumerate(tokblocks):
                        for (dcol, tt0, nn) in cols[bi]:
                            S.dma("sp", sqr, (lambda sq, dst_s, h0, dcol, tt0, nn: lambda e: e.dma_start(out=dst_s[h0:h0 + 4, :, dcol:dcol + nn].rearrange("h d t -> d h t"), in_=sq[:, :, tt0:tt0 + nn]))(sq, dst_s, h0, dcol, tt0, nn),
                                  reads=[sqr], writes=[("qT_s" if kind == "q" else "kT_s")], accum=True)
                elif kind == "v":
                    sv = stv[nv % 2]
                    svr = f"stv{nv % 2}"
                    nv += 1
                    for s in range(nsub):
                        b = accb % 4
                        accb += 1
                        pb = bank(b)
                        for kc in range(32):
                            S.op("pe", (lambda pb, W, kc, s, hT=hT: lambda e: e.matmul(pb, lhsT=hT[:, kc, s * 128:(s + 1) * 128], rhs=W[:, kc, :], start=(kc == 0), stop=(kc == 31)))(pb, W, kc, s),
                                 reads=[wres, "hT"], writes=[f"bank{b}"])
                        evac("act" if (accb % 2) else "dve", sv[:, s, :], pb, [f"bank{b}"], [svr])
                    for (row0, s0, ns) in vrows:
                        S.dma("sp", svr, (lambda sv, row0, s0, ns, idx: lambda e: e.dma_start(out=v_s[row0:row0 + ns * 128, idx * 512:(idx + 1) * 512].rearrange("(s p) c -> p s c", p=128), in_=sv[:, s0:s0 + ns, :]))(sv, row0, s0, ns, idx),
                              reads=[svr], writes=["v_s"], accum=True)
                else:
                    dst_s = gu_s if kind == "u" else gg_s
                    for s in range(nsub):
                        b = accb % 4
                        accb += 1
                        pb = bank(b)
                        for kc in range(32):
                            S.op("pe", (lambda pb, W, kc, s, hT=hT: lambda e: e.matmul(pb, lhsT=hT[:, kc, s * 128:(s + 1) * 128], rhs=W[:, kc, :], start=(kc == 0), stop=(kc == 31)))(pb, W, kc, s),
                                 reads=[wres, "hT"], writes=[f"bank{b}"])
                        sg = stg[ng % 4]
                        sgr = f"stg{ng % 4}"
                        ng += 1
                        S.op("act", (lambda sg, pb: lambda e: e.activation(out=sg[:], in_=pb, func=AF.Gelu_apprx_tanh))(sg, pb),
                             reads=[f"bank{b}"], writes=[sgr])
                        row0 = urow0 + s * 128
                        S.dma("sp", sgr, (lambda sg, dst_s, row0, idx: lambda e: e.dma_start(out=dst_s[row0:row0 + 128, idx * 512:(idx + 1) * 512], in_=sg[:]))(sg, dst_s, row0, idx),
                              reads=[sgr], writes=[("gu_s" if kind == "u" else "gg_s")], accum=True)
                if pi + 2 < npieces:
                    load_w(wsl[sl], f"w1sl{sl}", f"w1sl{sl}", win_v, colof(*pieces[pi + 2]))
                if adaq is not None:
                    tgt = adaq[0] + ((adaq[1] - adaq[0]) * (pi + 1) + npieces - 1) // npieces
                    while adan < min(tgt, adaq[1]):
                        ada1_quarter(adan, adaq[1])
                        adan += 1
            S.fence()

        piecesA = [("k", i) for i in range(4)] + [("v", i) for i in range(4)]
        p1_pass("A", xh, 6, lambda s: (7, 6) if s >= 4 else (1, 0), piecesA,
                tokblocks=[(0, 512), (512, 256)],
                kcol0=[[(0, 0, 256), (2304, 256, 256)], [(2560, 512, 256)]], qcol0=None,
                vrows=[(0, 0, 2), (2304, 2, 2), (2560, 4, 2)], urow0=None)
        piecesO = [(k, i) for k in ("q", "k", "v", "g", "u") for i in range(4)]
        for half in range(2):
            t00 = half * 1024
            p1_pass("BC"[half], xo[t00:t00 + 1024, :], 8, lambda s: (1, 0), piecesO,
                    tokblocks=[(0, 512), (512, 512)],
                    kcol0=[[(256 + t00, 0, 512)], [(256 + t00 + 512, 512, 512)]],
                    qcol0=[[(t00, 0, 512)], [(t00 + 512, 512, 512)]],
                    vrows=[(256 + t00, 0, 8)], urow0=t00, adaq=(half * 32, half * 32 + 32))
        if PH_STOP <= 1:
            return finish(nc, S, st, out)

        A = Arena(nc, PBASE)
        maskt = A.t("maskt", [128, 5, 768], F32)
        G2 = [dict(q=A.t(f"aq{i}", [128, 2, 2048], BF16), k=A.t(f"ak{i}", [128, 2, 2816], BF16),
                   v=A.t(f"av{i}", [128, 22, 256], BF16)) for i in range(2)]
        tabs2 = [A.t(f"tab{i}", [128, 5, 768], F32) for i in range(2)]
        Sb2 = [A.t(f"Sb{i}", [128, 1024], F32) for i in range(2)]
        Pb2 = [A.t(f"Pb{i}", [128, 1024], BF16) for i in range(2)]
        PT2 = [A.t(f"PT{i}", [128, 1024], BF16) for i in range(2)]
        st2 = [A.t(f"ast{i}", [128, 8], F32) for i in range(4)]
        ost2 = [A.t(f"ost{i}", [128, 16, 128], F32) for i in range(2)]
        cTs2 = A.t("cTs2", [128, 32, 2], F32)
        lhsE = A.t("lhsE", [128, 32, 128], BF16)
        wE = [A.t(f"wE{i}", [128, 16, 512], BF16) for i in range(2)]
        bE = [A.t(f"bE{i}", [128, 512], F32) for i in range(2)]
        nE = [A.t(f"nE{i}", [128, 512], F32) for i in range(2)]
        eE = [A.t(f"eE{i}", [128, 512], F32) for i in range(2)]
        S.dma("sp", "maskt", lambda e: e.dma_start(out=maskt[:], in_=maskx), writes=["maskt"])
        S.dma("sp", "cTs2", lambda e: e.dma_start(out=cTs2[:], in_=cT), writes=["cTs2"])
        S.op("act", lambda e: e.activation(out=cTs2[:], in_=cTs2[:], func=AF.Silu), reads=["cTs2"], writes=["cTs2"])
        S.op("dve", lambda e: e.tensor_copy(out=lhsE[:], in_=cTs2[:, :, 0:1].to_broadcast([128, 32, 128])), reads=["cTs2"], writes=["lhsE"])

        def ada_load(hp):
            j = 16 + hp // 2
            hf = hp % 2
            sl = hp % 2
            S.dma("pool", f"wE{sl}", lambda e: e.dma_start(out=wE[sl][:], in_=wada_v[:, hf * 16:(hf + 1) * 16, j * 512:(j + 1) * 512]), writes=[f"wE{sl}"])

        def ada_block(n):
            j = 16 + n
            which = j // 8
            cb = j % 8
            bsl = n % 2
            bcast_load(bE[bsl][:], f"bE{bsl}", f"bE{bsl}", b_ada[j * 512:(j + 1) * 512])
            if which == 4:
                bcast_load(nE[bsl][:], f"nE{bsl}", f"nE{bsl}", norm2_w[cb * 512:(cb + 1) * 512])
            pb = bank(7)
            for hf in range(2):
                hp = 2 * n + hf
                sl = hp % 2
                for k in range(16):
                    kc = hf * 16 + k
                    S.op("pe", (lambda sl, k, kc: lambda e: e.matmul(pb, lhsT=lhsE[:, kc, :], rhs=wE[sl][:, k, :], start=(kc == 0), stop=(kc == 31)))(sl, k, kc),
                         reads=[f"wE{sl}", "lhsE"], writes=["bank7"])
                if hp + 2 < 16:
                    ada_load(hp + 2)
            e_t = eE[bsl]
            S.op("dve", lambda e: e.tensor_tensor(out=e_t[:], in0=pb, in1=bE[bsl][:], op=ALU.add), reads=["bank7", f"bE{bsl}"], writes=[f"eE{bsl}"])
            if which == 4:
                S.op("dve", lambda e: e.scalar_tensor_tensor(out=e_t[:], in0=e_t[:], scalar=1.0, in1=nE[bsl][:], op0=ALU.add, op1=ALU.mult),
                     reads=[f"eE{bsl}", f"nE{bsl}"], writes=[f"eE{bsl}"])
            S.dma("sp", f"eE{bsl}", lambda e: e.dma_start(out=modbc[which, :, cb * 512:(cb + 1) * 512], in_=e_t[:]), reads=[f"eE{bsl}"], writes=["modbc"], accum=True)

        for hp in range(2):
            ada_load(hp)

        def pair_cfg(i):
            if i == 0:
                return 0, 0, 12
            if i == 1:
                return 1, 0, 12
            if i == 14:
                return 3, 28, 10
            if i == 15:
                return 4, 28, 12
            return 2, 2 * i, 10

        def attn_stages(t, G, gres, hh, tab, tabres, i, ost, ostres):
            ti, B, nr = pair_cfg(i)
            W = nr * 64
            rem = W - 512
            NC_ = W + 256
            nch = NC_ // 128
            vts = [B // 2 + c for c in range(W // 128)] + [20, 21]
            p2_ = t % 2
            p4_ = t % 4
            Sps = PS[p2_]
            Sr = f"S{p2_}"
            q_ap = G["q"][:, hh, i * 128:(i + 1) * 128]
            kk = G["k"]
            vv = G["v"]
            sb = Sb2[p2_]
            stt = st2[p4_]
            pb_ = Pb2[p2_]
            ptb = bank_bf(4 + p2_)
            pt_ = PT2[p2_]
            ob = bank(6)[:, p2_ * 128:(p2_ + 1) * 128]

            def s1():
                S.op("pe", lambda e: e.matmul(Sps[:, 0:512], lhsT=q_ap, rhs=kk[:, hh, B * 64:B * 64 + 512], start=True, stop=True),
                     reads=[gres + "q", gres + "k"], writes=[Sr])
                S.op("pe", lambda e: e.matmul(Sps[:, 512:512 + rem], lhsT=q_ap, rhs=kk[:, hh, (B + 8) * 64:(B + 8) * 64 + rem], start=True, stop=True),
                     reads=[gres + "q", gres + "k"], writes=[Sr])
                S.op("pe", lambda e: e.matmul(Sps[:, W:W + 256], lhsT=q_ap, rhs=kk[:, hh, 2560:2816], start=True, stop=True),
                     reads=[gres + "q", gres + "k"], writes=[Sr])
                S.op("dve", lambda e: e.scalar_tensor_tensor(out=sb[:, 0:W], in0=Sps[:, 0:W], scalar=SCALE, in1=tab[:, ti, 0:W], op0=ALU.mult, op1=ALU.add),
                     reads=[Sr, tabres], writes=[f"Sb{p2_}"])
                S.op("dve", lambda e: e.tensor_scalar_mul(out=sb[:, W:NC_], in0=Sps[:, W:NC_], scalar1=SCALE), reads=[Sr], writes=[f"Sbc{p2_}"])
                S.op("dve", lambda e: e.reduce_max(out=stt[:, 1:2], in_=sb[:, 0:NC_], axis=AX.X, negate=True), reads=[f"Sb{p2_}", f"Sbc{p2_}"], writes=[f"nmx{p4_}"])
                S.op("act", lambda e: e.activation(out=pb_[:, 0:NC_], in_=sb[:, 0:NC_], func=AF.Exp, bias=stt[:, 1:2], scale=1.0, accum_out=stt[:, 2:3]),
                     reads=[f"Sb{p2_}", f"Sbc{p2_}", f"nmx{p4_}"], writes=[f"Pb{p2_}", f"rs{p4_}"])

            def s2():
                for ci in range(nch):
                    S.op("pe", (lambda ci: lambda e: e.transpose(out=ptb[:, ci * 128:(ci + 1) * 128], in_=pb_[:, ci * 128:(ci + 1) * 128], identity=ident[:]))(ci),
                         reads=[f"Pb{p2_}", "ident"], writes=[f"bank{4 + p2_}"])
                S.op("act", lambda e: e.copy(out=pt_[:, 0:NC_], in_=ptb[:, 0:NC_]), reads=[f"bank{4 + p2_}"], writes=[f"PT{p2_}"])
                S.op("dve", lambda e: e.reciprocal(out=stt[:, 3:4], in_=stt[:, 2:3]), reads=[f"rs{p4_}"], writes=[f"rinv{p4_}"])

            def s3():
                for ci in range(nch):
                    S.op("pe", (lambda ci: lambda e: e.matmul(ob, lhsT=pt_[:, ci * 128:(ci + 1) * 128], rhs=vv[:, vts[ci], hh * 128:(hh + 1) * 128], start=(ci == 0), stop=(ci == nch - 1)))(ci),
                         reads=[f"PT{p2_}", gres + "v"], writes=["bank6"])
                S.op("act", lambda e: e.mul(out=ost[:, i, :], in_=ob, mul=stt[:, 3:4]), reads=["bank6", f"rinv{p4_}"], writes=[ostres])
            return s1, s2, s3

        def p2_load_group(gi):
            G = G2[gi % 2]
            gres = f"G{gi % 2}"
            S.dma("sp", gres + "q", lambda e: e.dma_start(out=G["q"][:], in_=qT_s[2 * gi:2 * gi + 2].rearrange("h d t -> d h t")), reads=["qT_s"], writes=[gres + "q"])
            S.dma("sp", gres + "k", lambda e: e.dma_start(out=G["k"][:], in_=kT_s[2 * gi:2 * gi + 2].rearrange("h d t -> d h t")), reads=["kT_s"], writes=[gres + "k"])
            S.dma("sp", gres + "v", lambda e: e.dma_start(out=G["v"][:], in_=v_s[:, gi * 256:(gi + 1) * 256].rearrange("(t p) c -> p t c", p=128)), reads=["v_s"], writes=[gres + "v"])

        def p2_load_tab(h):
            tab = tabs2[h % 2]
            tr = f"tab{h % 2}"
            S.dma("sp", tr, lambda e: e.dma_start(out=tab[:], in_=rpbx[h]), writes=[tr])
            S.op("pool", lambda e: e.tensor_tensor(out=tab[:], in0=tab[:], in1=maskt[:], op=ALU.add), reads=[tr, "maskt"], writes=[tr])

        iters = [(h // 2, h % 2, h, i) for h in range(NH) for i in range(16)]
        NIT = len(iters)
        stages = {}
        p2_load_group(0)
        p2_load_tab(0)
        nada = 0
        for t in range(NIT + 2):
            if t < NIT:
                gi, hh, h, i = iters[t]
                if i == 0:
                    if h + 1 < NH:
                        p2_load_tab(h + 1)
                if i == 2 and hh == 0 and gi + 1 < 8:
                    p2_load_group(gi + 1)
                stages[t] = attn_stages(t, G2[gi % 2], f"G{gi % 2}", hh, tabs2[h % 2], f"tab{h % 2}", i, ost2[h % 2], f"ost{h % 2}")
                stages[t][0]()
            if 0 <= t - 1 < NIT:
                stages[t - 1][1]()
            if 0 <= t - 2 < NIT:
                stages[t - 2][2]()
                gi_, hh_, h_, i_ = iters[t - 2]
                if i_ == 15:
                    S.dma("sp", f"ost{h_ % 2}", (lambda ost, h_: lambda e: e.dma_start(out=ona_s[:, h_ * 128:(h_ + 1) * 128].rearrange("(i p) c -> p i c", p=128), in_=ost[:]))(ost2[h_ % 2], h_),
                          reads=[f"ost{h_ % 2}"], writes=["ona_s"], accum=True)
                del stages[t - 2]
            if t % 28 == 6 and nada < 8:
                ada_block(nada)
                nada += 1
        while nada < 8:
            ada_block(nada)
            nada += 1
        S.fence()
        if PH_STOP <= 2:
            return finish(nc, S, st, out)

        A = Arena(nc, PBASE)
        ocT = A.t("ocT", [128, 32, 1024], BF16)
        wC = [A.t(f"wC{i}", [128, 32, 512], BF16) for i in range(2)]
        cn_sgu = A.t("cn_sgu", [128, DNA], F32)
        cn_na = A.t("cn_na", [128, DNA], F32)
        cn_gs = A.t("cn_gs", [128, DNA], F32)
        wsT_f = A.t("wsT_f", [128, 4, 128], F32)
        wsT = A.t("wsT", [128, 4, 128], BF16)
        sbT = A.t("sbT", [128, 4], F32)
        st4 = A.t("st4", [128, 8], F32)
        UB4 = A.off
        ona_t = A.t("ona_t", [128, DNA], F32)
        gu_t = A.t("gu_t", [128, DNA], F32)
        gg_t = A.t("gg_t", [128, DNA], F32)
        oc_tok = A.t("oc_tok", [128, D], BF16)
        gn_t = A.t("gn_t", [128, DNA], BF16)
        junk4 = A.t("junk4", [128, DNA], BF16)
        A2 = Arena(nc, UB4)
        xb4 = [A2.t(f"xb4_{i}", [128, 512], F32) for i in range(3)]
        tt4 = [A2.t(f"tt4_{i}", [128, 512], F32) for i in range(2)]
        g1b = [A2.t(f"g1b{i}", [128, 512], F32) for i in range(2)]
        junk4b = A2.t("junk4b", [128, 512], BF16)
        bcast_load(cn_sgu[:], "cn_sgu", "cn_sgu", sgu_nw)
        bcast_load(cn_na[:], "cn_na", "cn_na", gn_na)
        bcast_load(cn_gs[:], "cn_gs", "cn_gs", gn_sgu)
        S.dma("sp", "wsT_f", lambda e: e.dma_start(out=wsT_f[:], in_=sgu_wT), writes=["wsT_f"])
        S.dma("sp", "sbT", lambda e: e.dma_start(out=sbT[:], in_=sgu_bT), writes=["sbT"])
        S.op("dve", lambda e: e.tensor_copy(out=wsT[:], in_=wsT_f[:]), reads=["wsT_f"], writes=["wsT"])
        wout_v = _rows(w_out)

        def sq_rstd(src, srcres, junk, junkres, col, n):
            S.op("act", lambda e: e.activation(out=junk, in_=src, func=AF.Square, accum_out=st4[:, col:col + 1]), reads=[srcres], writes=[junkres, f"st4_{col}"])
            rstd_from_ss(st4[:, col:col + 1], st4[:, col + 1:col + 2], n, [f"st4_{col}"], [f"st4_{col + 1}"])

        def p4_prologue(t00):
            def ld(which, s):
                r0_ = t00 + s * 128
                if which == "gg":
                    S.dma("sp", "gg_t", lambda e: e.dma_start(out=gg_t[:], in_=gg_s[r0_:r0_ + 128, :]), reads=["gg_s"], writes=["gg_t"])
                elif which == "gu":
                    S.dma("sp", "gu_t", lambda e: e.dma_start(out=gu_t[:], in_=gu_s[r0_:r0_ + 128, :]), reads=["gu_s"], writes=["gu_t"])
                else:
                    S.dma("sp", "ona_t", lambda e: e.dma_start(out=ona_t[:], in_=ona_s[r0_:r0_ + 128, :]), reads=["ona_s"], writes=["ona_t"])

            def a1(s):
                sq_rstd(gg_t[:], "gg_t", junk4[:], "junk4", 0, DNA)
                S.op("dve", lambda e: e.scalar_tensor_tensor(out=gn_t[:], in0=gg_t[:], scalar=st4[:, 1:2], in1=cn_sgu[:], op0=ALU.mult, op1=ALU.mult),
                     reads=["gg_t", "st4_1", "cn_sgu"], writes=["gn_t"])
                if s + 1 < 8:
                    ld("gg", s + 1)
                sq_rstd(ona_t[:], "ona_t", junk4[:], "junk4", 4, DNA)
                for g in range(4):
                    b = g % 2
                    pb = bank(b)
                    S.op("pe", (lambda pb, g: lambda e: e.matmul(pb, lhsT=wsT[:, g, :], rhs=gn_t[:, g * 512:(g + 1) * 512], start=True, stop=True))(pb, g),
                         reads=["wsT", "gn_t"], writes=[f"bank{b}"])
                    S.op("dve", (lambda pb, g: lambda e: e.scalar_tensor_tensor(out=gu_t[:, g * 512:(g + 1) * 512], in0=pb, scalar=sbT[:, g:g + 1], in1=gu_t[:, g * 512:(g + 1) * 512], op0=ALU.add, op1=ALU.mult))(pb, g),
                         reads=[f"bank{b}", "sbT", "gu_t"], writes=["gu_t"])
                sq_rstd(gu_t[:], "gu_t", junk4[:], "junk4", 2, DNA)

            def a2(s):
                S.op("dve", lambda e: e.scalar_tensor_tensor(out=oc_tok[:, 0:DNA], in0=ona_t[:], scalar=st4[:, 5:6], in1=cn_na[:], op0=ALU.mult, op1=ALU.mult),
                     reads=["ona_t", "st4_5", "cn_na"], writes=["oc_lo"])
                if s + 1 < 8:
                    ld("ona", s + 1)
                S.op("dve", lambda e: e.scalar_tensor_tensor(out=oc_tok[:, DNA:D], in0=gu_t[:], scalar=st4[:, 3:4], in1=cn_gs[:], op0=ALU.mult, op1=ALU.mult),
                     reads=["gu_t", "st4_3", "cn_gs"], writes=["oc_hi"])
                if s + 1 < 8:
                    ld("gu", s + 1)

            def b_(s):
                transpose_tile(oc_tok, (lambda g: "oc_lo" if g < 2 else "oc_hi"), ocT, "ocT", s * 128, 32, (6, 7))

            ld("gg", 0)
            ld("ona", 0)
            ld("gu", 0)
            for s in range(9):
                if s < 8:
                    a1(s)
                if s >= 1:
                    b_(s - 1)
                if s < 8:
                    a2(s)

        accb = 0
        nx = 0
        for half in range(2):
            t00 = half * 1024
            for p_ in range(2):
                load_w(wC[p_], f"wC{p_}", f"wC{p_}", wout_v, p_ * 512)
            p4_prologue(t00)
            S.fence()
            for ob_ in range(8):
                sl = ob_ % 2
                gb = g1b[ob_ % 2]
                gbr = f"g1b{ob_ % 2}"
                S.dma("sp", gbr, (lambda gb, ob_: lambda e: e.dma_start(out=gb[:], in_=modbc[2, :, ob_ * 512:(ob_ + 1) * 512]))(gb, ob_), reads=["modbc"], writes=[gbr])
                W = wC[sl]
                for s in range(8):
                    b = accb % 4
                    accb += 1
                    pb = bank(b)
                    for kc in range(32):
                        S.op("pe", (lambda pb, W, kc, s: lambda e: e.matmul(pb, lhsT=ocT[:, kc, s * 128:(s + 1) * 128], rhs=W[:, kc, :], start=(kc == 0), stop=(kc == 31)))(pb, W, kc, s),
                             reads=[f"wC{sl}", "ocT"], writes=[f"bank{b}"])
                    xb = xb4[nx % 3]
                    xbr = f"xb4_{nx % 3}"
                    tt = tt4[nx % 2]
                    ttr = f"tt4_{nx % 2}"
                    nx += 1
                    r0_ = t00 + s * 128
                    tsub = half * 8 + s
                    S.dma("sp", xbr, (lambda xb, r0_, ob_: lambda e: e.dma_start(out=xb[:], in_=xo[r0_:r0_ + 128, ob_ * 512:(ob_ + 1) * 512]))(xb, r0_, ob_), writes=[xbr])
                    S.op("dve", (lambda tt, pb, gb: lambda e: e.tensor_tensor(out=tt[:], in0=pb, in1=gb[:], op=ALU.mult))(tt, pb, gb), reads=[f"bank{b}", gbr], writes=[ttr])
                    S.op("pool", (lambda xb, tt: lambda e: e.tensor_tensor(out=xb[:], in0=tt[:], in1=xb[:], op=ALU.add))(xb, tt), reads=[ttr, xbr], writes=[xbr])
                    S.op("act", (lambda xb, tsub, ob_: lambda e: e.activation(out=junk4b[:], in_=xb[:], func=AF.Square, accum_out=stat1[:, tsub, ob_:ob_ + 1]))(xb, tsub, ob_),
                         reads=[xbr], writes=["junk4b", "stat1"])
                    S.dma("sp", xbr, (lambda xb, r0_, ob_: lambda e: e.dma_start(out=x1_s[r0_:r0_ + 128, ob_ * 512:(ob_ + 1) * 512], in_=xb[:]))(xb, r0_, ob_), reads=[xbr], writes=["x1_s"], accum=True)
                if ob_ + 2 < 8:
                    load_w(wC[sl], f"wC{sl}", f"wC{sl}", wout_v, (ob_ + 2) * 512)
            S.fence()
        S.op("dve", lambda e: e.reduce_sum(out=rstd2[:], in_=stat1[:], axis=AX.X), reads=["stat1"], writes=["rstd2"])
        rstd_from_ss(rstd2[:], rstd2[:], D, ["rstd2"], ["rstd2"])
        S.fence()
        if PH_STOP <= 3:
            return finish(nc, S, st, out)

        A = Arena(nc, PBASE)
        h2T = A.t("h2T", [128, 32, 1024], BF16)
        OV5 = A.off
        gwb5 = A.t("gwb5", [128, D], F32)
        shb5 = A.t("shb5", [128, D], F32)
        AO = Arena(nc, OV5)
        wE5 = [AO.t(f"wE5_{i}", [128, 8, 512], BF16) for i in range(2)]
        lhsE5 = AO.t("lhsE5", [128, 32, 128], BF16)
        bE5 = [AO.t(f"bE5_{i}", [128, 512], F32) for i in range(2)]
        eE5 = [AO.t(f"eE5_{i}", [128, 512], F32) for i in range(2)]
        assert AO.off <= A.off
        cTs5 = A.t("cTs5", [128, 32, 2], F32)

        def ada5_load(q):
            j = 40 + q // 4
            qt = q % 4
            sl = q % 2
            S.dma("pool", f"wE5_{sl}", lambda e: e.dma_start(out=wE5[sl][:], in_=wada_v[:, qt * 8:(qt + 1) * 8, j * 512:(j + 1) * 512]), writes=[f"wE5_{sl}"])

        def ada5_quarter(q, qend):
            n = q // 4
            qt = q % 4
            j = 40 + n
            cb = j % 8
            bsl = n % 2
            sl = q % 2
            pb = bank(4)
            if qt == 0:
                bcast_load(bE5[bsl][:], f"bE5_{bsl}", f"bE5_{bsl}", b_ada[j * 512:(j + 1) * 512])
            for k in range(8):
                kc = qt * 8 + k
                S.op("pe", (lambda k, kc: lambda e: e.matmul(pb, lhsT=lhsE5[:, kc, :], rhs=wE5[sl][:, k, :], start=(kc == 0), stop=(kc == 31)))(k, kc),
                     reads=[f"wE5_{sl}", "lhsE5"], writes=["bank4"])
            if q + 2 < qend:
                ada5_load(q + 2)
            if qt == 3:
                e_t = eE5[bsl]
                S.op("dve", lambda e: e.tensor_tensor(out=e_t[:], in0=pb, in1=bE5[bsl][:], op=ALU.add), reads=["bank4", f"bE5_{bsl}"], writes=[f"eE5_{bsl}"])
                S.dma("sp", f"eE5_{bsl}", lambda e: e.dma_start(out=modbc[5, :, cb * 512:(cb + 1) * 512], in_=e_t[:]), reads=[f"eE5_{bsl}"], writes=["modbc"], accum=True)

        wD = [A.t(f"wD{i}", [128, 32, 512], BF16) for i in range(2)]
        UB5 = A.off
        xs5 = [A.t(f"xs5_{i}", [128, D], F32) for i in range(2)]
        hb5 = A.t("hb5", [128, D], BF16)
        A2 = Arena(nc, UB5)
        rl5 = [A2.t(f"rl5_{i}", [128, 512], F32) for i in range(3)]
        hst5 = [A2.t(f"hst5_{i}", [128, 4, 1024], BF16) for i in range(2)]
        wff1_v = _rows(w_ff1)
        accb = 0
        nr5 = 0
        for half in range(2):
            t00 = half * 1024
            for p_ in range(2):
                load_w(wD[p_], f"wD{p_}", f"wD{p_}", wff1_v, p_ * 512)
            prologue(x1_s[t00:t00 + 1024, :], 8, lambda s: (4, 3), h2T, gwb5, shb5, xs5, hb5, None,
                     rstd_ext=(lambda half: lambda s: rstd2[:, half * 8 + s:half * 8 + s + 1])(half), force_reload=True)
            S.fence()
            S.dma("sp", "cTs5", lambda e: e.dma_start(out=cTs5[:], in_=cT), writes=["cTs5"])
            S.op("act", lambda e: e.activation(out=cTs5[:], in_=cTs5[:], func=AF.Silu), reads=["cTs5"], writes=["cTs5"])
            S.op("dve", lambda e: e.tensor_copy(out=lhsE5[:], in_=cTs5[:, :, 0:1].to_broadcast([128, 32, 128])), reads=["cTs5"], writes=["lhsE5"])
            q0 = half * 16
            ada5_load(q0)
            ada5_load(q0 + 1)
            for pi in range(32):
                sl = pi % 2
                if pi % 2 == 1:
                    ada5_quarter(q0 + pi // 2, q0 + 16)
                W = wD[sl]
                hs = hst5[pi % 2]
                hsr = f"hst5_{pi % 2}"
                for hb_ in range(4):
                    for tb in range(2):
                        b = accb % 4
                        accb += 1
                        pb = bank(b)
                        for kc in range(32):
                            S.op("pe", (lambda pb, W, kc, hb_, tb: lambda e: e.matmul(pb, lhsT=W[:, kc, hb_ * 128:(hb_ + 1) * 128], rhs=h2T[:, kc, tb * 512:(tb + 1) * 512], start=(kc == 0), stop=(kc == 31)))(pb, W, kc, hb_, tb),
                                 reads=[f"wD{sl}", "hT"], writes=[f"bank{b}"])
                        rl = rl5[nr5 % 3]
                        rlr = f"rl5_{nr5 % 3}"
                        nr5 += 1
                        S.op("act", (lambda rl, pb: lambda e: e.activation(out=rl[:], in_=pb, func=AF.Relu))(rl, pb), reads=[f"bank{b}"], writes=[rlr])
                        eng = "pool" if (nr5 % 2) else "dve"
                        S.op(eng, (lambda rl, hs, hb_, tb: lambda e: e.tensor_tensor(out=hs[:, hb_, tb * 512:(tb + 1) * 512], in0=rl[:], in1=rl[:], op=ALU.mult))(rl, hs, hb_, tb),
                             reads=[rlr], writes=[hsr])
                S.dma("sp", hsr, (lambda hs, pi, t00: lambda e: e.dma_start(out=hid_s[pi * 512:(pi + 1) * 512, t00:t00 + 1024].rearrange("(hb p) t -> p hb t", p=128), in_=hs[:]))(hs, pi, t00),
                      reads=[hsr], writes=["hid_s"], accum=True)
                if pi + 2 < 32:
                    load_w(wD[sl], f"wD{sl}", f"wD{sl}", wff1_v, (pi + 2) * 512)
            S.fence()
        if PH_STOP <= 4:
            return finish(nc, S, st, out)

        A = Arena(nc, PBASE)
        yacc = A.t("yacc", [128, 16, 1024], F32)
        hc6 = [A.t(f"hc6_{i}", [128, 8, 2048], BF16) for i in range(2)]
        w26 = [A.t(f"w26_{i}", [128, 8, 1024], BF16) for i in range(2)]
        x1b6 = [A.t(f"x1b6_{i}", [128, 1024], F32) for i in range(4)]
        tt6 = [A.t(f"tt6_{i}", [128, 1024], F32) for i in range(2)]
        g2b6 = [A.t(f"g2b6_{i}", [128, 1024], F32) for i in range(2)]
        junk6 = A.t("junk6", [128, 1024], BF16)
        wff2_v = _rows(w_ff2)
        accb = 0

        def p5b_load(cp, fc, sl):
            n0 = cp * 1024
            S.dma("sp", f"hc6_{sl}", lambda e: e.dma_start(out=hc6[sl][:], in_=hid_s[fc * 1024:(fc + 1) * 1024, :].rearrange("(kc p) t -> p kc t", p=128)),
                  reads=["hid_s"], writes=[f"hc6_{sl}"])
            load_w(w26[sl], f"w26_{sl}", f"w26_{sl}", wff2_v, n0, nkc=8, k0=fc * 8, ncol=1024)

        chunks6 = [(cp, fc) for cp in range(4) for fc in range(16)]
        for ci_ in range(2):
            p5b_load(chunks6[ci_][0], chunks6[ci_][1], ci_ % 2)

        def ep_load(e_):
            cp_, s_ = e_ // 16, e_ % 16
            xb = x1b6[e_ % 4]
            xbr = f"x1b6_{e_ % 4}"
            S.dma("sp", xbr, lambda e: e.dma_start(out=xb[:], in_=x1_s[s_ * 128:(s_ + 1) * 128, cp_ * 1024:(cp_ + 1) * 1024]), reads=["x1_s"], writes=[xbr])

        def epilogue6(e_):
            cp_, s_ = e_ // 16, e_ % 16
            xb = x1b6[e_ % 4]
            xbr = f"x1b6_{e_ % 4}"
            tt = tt6[e_ % 2]
            ttr = f"tt6_{e_ % 2}"
            g2b = g2b6[cp_ % 2]
            g2r = f"g2b6_{cp_ % 2}"
            ya = yacc[:, s_, :]
            yrs = [f"y{s_}_0", f"y{s_}_1"]
            S.op("dve", lambda e: e.tensor_tensor(out=tt[:], in0=ya, in1=g2b[:], op=ALU.mult), reads=yrs + [g2r], writes=[ttr])
            S.op("pool", lambda e: e.tensor_tensor(out=xb[:], in0=tt[:], in1=xb[:], op=ALU.add), reads=[ttr, xbr], writes=[xbr])

        def epilogue6b(e_):
            cp_, s_ = e_ // 16, e_ % 16
            xb = x1b6[e_ % 4]
            xbr = f"x1b6_{e_ % 4}"
            S.op("act", lambda e: e.activation(out=junk6[:], in_=xb[:], func=AF.Square, accum_out=stat2[:, s_, cp_:cp_ + 1]), reads=[xbr], writes=["junk6", "stat2"])
            S.dma("sp", xbr, lambda e: e.dma_start(out=x2_s[s_ * 128:(s_ + 1) * 128, cp_ * 1024:(cp_ + 1) * 1024], in_=xb[:]), reads=[xbr], writes=["x2_s"], accum=True)
            if e_ + 4 < 64 and (e_ + 4) // 16 == cp_:
                ep_load(e_ + 4)

        for ci_, (cp, fc) in enumerate(chunks6):
            n0 = cp * 1024
            g2b = g2b6[cp % 2]
            g2r = f"g2b6_{cp % 2}"
            if fc == 0:
                S.dma("sp", g2r, (lambda g2b, n0: lambda e: e.dma_start(out=g2b[:], in_=modbc[5, :, n0:n0 + 1024]))(g2b, n0), reads=["modbc"], writes=[g2r])
            if fc == 15:
                for e_ in range(cp * 16, cp * 16 + 4):
                    ep_load(e_)
            sl = ci_ % 2
            hc = hc6[sl]
            w2 = w26[sl]
            for s in range(16):
                if fc == 0 and cp > 0:
                    epilogue6((cp - 1) * 16 + s)
                    if s >= 1:
                        epilogue6b((cp - 1) * 16 + s - 1)
                for cb in range(2):
                    b = accb % 4
                    accb += 1
                    pb = bank(b)
                    for kc in range(8):
                        S.op("pe", (lambda pb, hc, w2, kc, s, cb: lambda e: e.matmul(pb, lhsT=hc[:, kc, s * 128:(s + 1) * 128], rhs=w2[:, kc, cb * 512:(cb + 1) * 512], start=(kc == 0), stop=(kc == 7)))(pb, hc, w2, kc, s, cb),
                             reads=[f"hc6_{sl}", f"w26_{sl}"], writes=[f"bank{b}"])
                    ya = yacc[:, s, cb * 512:(cb + 1) * 512]
                    yr = f"y{s}_{cb}"
                    if fc == 0:
                        S.op("act", (lambda ya, pb: lambda e: e.copy(out=ya, in_=pb))(ya, pb), reads=[f"bank{b}"], writes=[yr])
                    else:
                        S.op("dve", (lambda ya, pb: lambda e: e.tensor_tensor(out=ya, in0=pb, in1=ya, op=ALU.add))(ya, pb), reads=[f"bank{b}", yr], writes=[yr])
            if fc == 0 and cp > 0:
                epilogue6b((cp - 1) * 16 + 15)
            if ci_ + 2 < len(chunks6):
                p5b_load(chunks6[ci_ + 2][0], chunks6[ci_ + 2][1], sl)
        for s in range(16):
            epilogue6(48 + s)
            if s >= 1:
                epilogue6b(48 + s - 1)
        epilogue6b(63)
        S.fence()
        S.op("dve", lambda e: e.reduce_sum(out=rstd3[:], in_=stat2[:], axis=AX.X), reads=["stat2"], writes=["rstd3"])
        rstd_from_ss(rstd3[:], rstd3[:], D, ["rstd3"], ["rstd3"])
        S.fence()
        if PH_STOP <= 5:
            return finish(nc, S, st, out)

        A = Arena(nc, PBASE)
        fnb = A.t("fnb", [128, D], F32)
        x7 = [A.t(f"x7_{i}", [128, D], F32) for i in range(6)]
        bcast_load(fnb[:], "fnb", "fnb", fnw)
        for s in range(16):
            xt = x7[s % 6]
            xr = f"x7_{s % 6}"
            S.dma(("sp" if s % 2 == 0 else "pool"), xr, (lambda xt, s: lambda e: e.dma_start(out=xt[:], in_=x2_s[s * 128:(s + 1) * 128, :]))(xt, s), reads=["x2_s"], writes=[xr])
            S.op("dve", (lambda xt, s: lambda e: e.scalar_tensor_tensor(out=xt[:], in0=xt[:], scalar=rstd3[:, s:s + 1], in1=fnb[:], op0=ALU.mult, op1=ALU.mult))(xt, s),
                 reads=[xr, "rstd3", "fnb"], writes=[xr])
            S.dma("act", xr, (lambda xt, s: lambda e: e.dma_start(out=out[s * 128:(s + 1) * 128, :], in_=xt[:]))(xt, s), reads=[xr], writes=["out"], accum=True)
        return finish(nc, S, st, out)


def finish(nc, S, st, out):
    S.fence()
    stats = S.emit(st)
    print("sched stats", stats, flush=True)
    return nc


TABS = [(0, 0, 12), (1, 0, 12), (2, 4, 9), (14, 28, 9), (15, 28, 12)]


def _bias_index(half):
    R0 = 32 * half
    p = np.arange(128)
    hq = p // 64
    c = p % 64
    m = np.arange(768)
    irel = m // 64
    j = m % 64
    ro = np.zeros((5, 128, 768), np.int64)
    co = np.zeros((5, 128, 768), np.int64)
    valid = np.zeros((5, 128, 768), bool)
    for t, (i, B, nrows) in enumerate(TABS):
        sq = 4 + 2 * i + hq
        gq = R0 - 4 + sq
        r0 = np.clip(gq - 4, 0, 56)
        cs = np.clip(c - 8, 0, 48)
        gk = R0 - 4 + B + irel
        v = (irel[None, :] < nrows) & (gk[None, :] >= r0[:, None]) & (gk[None, :] < r0[:, None] + 8) \
            & (j[None, :] >= cs[:, None]) & (j[None, :] < cs[:, None] + 16)
        valid[t] = v
        ro[t] = np.clip(gk[None, :] - gq[:, None] + 7, 0, 14)
        co[t] = np.clip(j[None, :] - c[:, None] + 15, 0, 30)
    return ro, co, valid


_PROG = {}


def make_in_maps(inp):
    x = np.asarray(inp["x"], np.float32)
    c = np.asarray(inp["c"], np.float32)
    ctx = np.asarray(inp["ctx"], np.float32)
    c_ctx = np.asarray(inp["c_ctx"], np.float32)
    rpb = np.asarray(inp["rpb"], np.float32)[0]
    g = lambda k: np.ascontiguousarray(np.asarray(inp[k], np.float32)[0])
    shared = {
        "w_ada": g("w_ada"), "b_ada": g("b_ada"), "norm1_w": g("norm1_w"), "w_in": g("w_in"),
        "sgu_nw": g("sgu_norm_w"),
        "sgu_wT": np.ascontiguousarray(np.transpose(g("sgu_w"), (2, 0, 1))),
        "sgu_bT": np.ascontiguousarray(g("sgu_b").T),
        "gn_na": g("grp_norm_na"), "gn_sgu": g("grp_norm_sgu"), "w_out": g("w_out"),
        "norm2_w": g("norm2_w"), "w_ff1": g("w_ff1"), "w_ff2": g("w_ff2"),
        "fnw": np.ascontiguousarray(np.asarray(inp["final_norm_w"], np.float32)),
        "identf": np.eye(128, dtype=np.float32),
    }
    tabs = []
    for half in range(2):
        ro, co, valid = _bias_index(half)
        rx = np.ascontiguousarray(np.transpose(rpb[:, ro, co], (0, 2, 1, 3)))
        mk = np.ascontiguousarray(np.transpose(np.where(valid, np.float32(0.0), np.float32(NEG)), (1, 0, 2)))
        tabs.append((rx, mk.astype(np.float32)))
    maps = []
    for core in range(8):
        b, half = core // 2, core % 2
        R0 = 32 * half
        xg = x[b].reshape(64, 64, D)
        pre = np.clip(np.arange(R0 - 4, R0), 0, 63)
        post = np.clip(np.arange(R0 + 32, R0 + 36), 0, 63)
        xo = np.ascontiguousarray(xg[R0:R0 + 32].reshape(2048, D))
        xh = np.ascontiguousarray(np.concatenate([xg[pre].reshape(256, D), xg[post].reshape(256, D), ctx[b]], 0))
        cT = np.ascontiguousarray(np.stack([c[b], c_ctx], -1).reshape(32, 128, 2).transpose(1, 0, 2))
        m = dict(shared)
        m.update({"xo": xo, "xh": xh, "cT": cT, "rpbx": tabs[half][0], "maskx": tabs[half][1]})
        maps.append(m)
    return maps


def kernel(**inp):
    if "nc" not in _PROG:
        _PROG["nc"] = build_program()
    nc = _PROG["nc"]
    maps = make_in_maps(inp)
    res = run_bass_kernel_spmd(nc, maps, core_ids=list(range(8)))
    _PROG["res"] = res
    outp = np.empty((4, 4096, D), np.float32)
    for core in range(8):
        b, half = core // 2, core % 2
        outp[b, half * 2048:(half + 1) * 2048, :] = np.asarray(res.results[core]["out"], np.float32)
    return outp
```
